# Optimizing a Trainium2 kernel written in Bass

```python
import jax, jax.numpy as jnp
from jax import lax
import numpy as np


D_MODEL = 2048
BATCH = 4
SEQ = 2048
DEPTH = 2

HEAD_DIM = 128
ROPE_THETA = 500000.0
ROPE_DIM = HEAD_DIM // 4
NORM_EPS = 1e-6
NEG_INF = -1e30

NSA_HEADS = 8
NSA_KV_HEADS = 2
NSA_CMP_LEN = 32
NSA_CMP_STRIDE = 16
NSA_CMP_HIDDEN = 256
NSA_SEL_BLOCK = 64
NSA_SEL_TOPN = 16
NSA_WINDOW = 512
NSA_Q_CHUNK = 64
NSA_FORCE_BONUS = 1e4

DIL_GROUPS = ((128, 1), (512, 4), (2048, 16))
DIL_HEADS_PER_GROUP = 4
DIL_HEADS = DIL_HEADS_PER_GROUP * len(DIL_GROUPS)
DIL_Q_CHUNK = 128

MOBA_HEADS = 8
MOBA_BLOCK = 256
MOBA_TOPK = 3
MOBA_Q_CHUNK = 32

BAND_BLOCK = 128
D_FF = 4 * D_MODEL

A_Q = NSA_HEADS * HEAD_DIM
A_KV = NSA_KV_HEADS * HEAD_DIM
A_G = 3 * NSA_HEADS
B_QKV = DIL_HEADS * HEAD_DIM
C_QKV = MOBA_HEADS * HEAD_DIM
A_OUT = NSA_HEADS * HEAD_DIM
B_OUT = DIL_HEADS_PER_GROUP * HEAD_DIM
C_OUT = MOBA_HEADS * HEAD_DIM
IN_SPLIT_SIZES = (A_Q, A_KV, A_KV, A_KV, A_KV, A_KV, A_KV, A_G,
                  B_QKV, B_QKV, B_QKV, C_QKV, C_QKV, C_QKV,
                  D_MODEL, D_MODEL, D_MODEL)
IN_WIDTH = sum(IN_SPLIT_SIZES)

kernel_name = "nsa_dilated_moba_gated_hybrid"


def rms_norm(x, g):
    xf = x.astype(jnp.float32)
    y = xf * lax.rsqrt(jnp.mean(xf * xf, axis=-1, keepdims=True) + NORM_EPS)
    return (y * g.astype(jnp.float32)).astype(x.dtype)


def rope_tables(seq):
    inv = ROPE_THETA ** (-jnp.arange(0, ROPE_DIM, 2, dtype=jnp.float32) / ROPE_DIM)
    ang = jnp.arange(seq, dtype=jnp.float32)[:, None] * inv[None, :]
    return jnp.cos(ang), jnp.sin(ang)


def apply_partial_rope(x, cos, sin):
    half = ROPE_DIM // 2
    xf = x[..., :ROPE_DIM].astype(jnp.float32)
    x1, x2 = xf[..., :half], xf[..., half:]
    c = cos[None, :, None, :]
    s = sin[None, :, None, :]
    rot = jnp.concatenate([x1 * c - x2 * s, x2 * c + x1 * s], axis=-1).astype(x.dtype)
    return jnp.concatenate([rot, x[..., ROPE_DIM:]], axis=-1)


def masked_softmax(scores, mask):
    s = jnp.where(mask, scores, NEG_INF)
    m = jnp.max(s, axis=-1, keepdims=True)
    e = jnp.where(mask, jnp.exp(s - m), 0.0)
    den = jnp.maximum(jnp.sum(e, axis=-1, keepdims=True), 1e-30)
    return e / den, (m + jnp.log(den))[..., 0]


def banded_causal_attention(q, k, v, window):
    B, S, H, dh = q.shape
    G = k.shape[2]
    R = H // G
    nb = S // BAND_BLOCK
    span = window + BAND_BLOCK
    kp = jnp.pad(k, ((0, 0), (window, 0), (0, 0), (0, 0)))
    vp = jnp.pad(v, ((0, 0), (window, 0), (0, 0), (0, 0)))
    idx = np.arange(nb)[:, None] * BAND_BLOCK + np.arange(span)[None, :]
    kb = kp[:, idx]
    vb = vp[:, idx]
    qb = q.reshape(B, nb, BAND_BLOCK, G, R, dh)
    s = jnp.einsum('bitgrd,bisgd->bigrts', qb, kb, preferred_element_type=jnp.float32) * dh ** -0.5
    tpos = np.arange(nb)[:, None] * BAND_BLOCK + np.arange(BAND_BLOCK)[None, :]
    kpos = idx - window
    mask = ((kpos[:, None, :] <= tpos[:, :, None]) & (kpos[:, None, :] > tpos[:, :, None] - window)
            & (kpos[:, None, :] >= 0))
    p, _ = masked_softmax(s, mask[None, :, None, None])
    o = jnp.einsum('bigrts,bisgd->bitgrd', p.astype(vb.dtype), vb)
    return o.reshape(B, S, H, dh)


def nsa_compress(kv, pe, w1, w2):
    S = kv.shape[1]
    M = (S - NSA_CMP_LEN) // NSA_CMP_STRIDE + 1
    idx = np.arange(M)[:, None] * NSA_CMP_STRIDE + np.arange(NSA_CMP_LEN)[None, :]
    blocks = kv[:, idx] + pe[None, None, :, None, :]
    hid = jax.nn.gelu(jnp.einsum('bmlgd,ldf->bmgf', blocks, w1))
    return jnp.einsum('bmgf,fd->bmgd', hid, w2)


def cmp_to_sel_overlap(seq):
    M = (seq - NSA_CMP_LEN) // NSA_CMP_STRIDE + 1
    NB = seq // NSA_SEL_BLOCK
    cs = np.arange(M)[:, None] * NSA_CMP_STRIDE
    bs = np.arange(NB)[None, :] * NSA_SEL_BLOCK
    ov = np.clip(np.minimum(cs + NSA_CMP_LEN, bs + NSA_SEL_BLOCK) - np.maximum(cs, bs), 0, None) / NSA_CMP_LEN
    return jnp.asarray(ov, dtype=jnp.float32)


def nsa_mixer(q, k_cmp, v_cmp, k_sel, v_sel, k_win, v_win, gates,
              pe_k, w1_k, w2_k, pe_v, w1_v, w2_v, cos, sin):
    B, S, H, dh = q.shape
    G = NSA_KV_HEADS
    R = H // G
    scale = dh ** -0.5
    t_pos = jnp.arange(S)

    kc = nsa_compress(k_cmp, pe_k, w1_k, w2_k)
    vc = nsa_compress(v_cmp, pe_v, w1_v, w2_v)
    M = kc.shape[1]
    qg = q.reshape(B, S, G, R, dh)
    s_c = jnp.einsum('btgrd,bmgd->btgrm', qg, kc, preferred_element_type=jnp.float32) * scale
    cmp_end = jnp.arange(M) * NSA_CMP_STRIDE + NSA_CMP_LEN - 1
    mask_c = (cmp_end[None, :] <= t_pos[:, None])[None, :, None, None, :]
    p_c, _ = masked_softmax(s_c, mask_c)
    o_cmp = jnp.einsum('btgrm,bmgd->btgrd', p_c.astype(vc.dtype), vc).reshape(B, S, H, dh)

    NB = S // NSA_SEL_BLOCK
    imp = jnp.einsum('btgrm,mj->btgj', p_c, cmp_to_sel_overlap(S))
    blk = jnp.arange(NB)
    q_blk = t_pos // NSA_SEL_BLOCK
    causal_blk = blk[None, :] <= q_blk[:, None]
    forced = (blk[None, :] == 0) | (blk[None, :] == q_blk[:, None]) | (blk[None, :] == q_blk[:, None] - 1)
    bonus = jnp.where(forced, NSA_FORCE_BONUS, 0.0)
    score = jnp.where(causal_blk[None, :, None, :], imp + bonus[None, :, None, :], NEG_INF)
    n_sel = min(NSA_SEL_TOPN, NB)
    top_s, top_i = lax.top_k(score, n_sel)
    top_ok = top_s > NEG_INF * 0.5

    qr = apply_partial_rope(q, cos, sin)
    ks = apply_partial_rope(k_sel, cos, sin).reshape(B, NB, NSA_SEL_BLOCK, G, dh).transpose(0, 3, 1, 2, 4)
    vs = v_sel.reshape(B, NB, NSA_SEL_BLOCK, G, dh).transpose(0, 3, 1, 2, 4)
    bi = jnp.arange(B)[:, None, None, None]
    gi = jnp.arange(G)[None, None, :, None]
    Qc = NSA_Q_CHUNK
    nkey = n_sel * NSA_SEL_BLOCK

    def sel_chunk(c):
        t0 = c * Qc
        tc = t0 + jnp.arange(Qc)
        qc = lax.dynamic_slice_in_dim(qr, t0, Qc, axis=1).reshape(B, Qc, G, R, dh)
        ic = lax.dynamic_slice_in_dim(top_i, t0, Qc, axis=1)
        okc = lax.dynamic_slice_in_dim(top_ok, t0, Qc, axis=1)
        kg = ks[bi, gi, ic]
        vg = vs[bi, gi, ic]
        s = jnp.einsum('btgrd,btgnld->btgrnl', qc, kg, preferred_element_type=jnp.float32) * scale
        kpos = ic[..., None] * NSA_SEL_BLOCK + jnp.arange(NSA_SEL_BLOCK)
        mask = (kpos <= tc[None, :, None, None, None]) & okc[..., None]
        p, _ = masked_softmax(s.reshape(B, Qc, G, R, nkey), mask.reshape(B, Qc, G, 1, nkey))
        o = jnp.einsum('btgrk,btgkd->btgrd', p.astype(vg.dtype), vg.reshape(B, Qc, G, nkey, dh))
        return o.reshape(B, Qc, H, dh)

    o_sel = lax.map(sel_chunk, jnp.arange(S // Qc))
    o_sel = o_sel.transpose(1, 0, 2, 3, 4).reshape(B, S, H, dh)

    o_win = banded_causal_attention(qr, apply_partial_rope(k_win, cos, sin), v_win, NSA_WINDOW)

    return gates[..., 0:1] * o_cmp + gates[..., 1:2] * o_sel + gates[..., 2:3] * o_win


def dilated_group_attention(q, k, v, window, dilation):
    B, S, Hg, dh = q.shape
    nk = window // dilation + 1
    Qc = DIL_Q_CHUNK
    offs = dilation * jnp.arange(nk)
    scale = dh ** -0.5

    def chunk(c):
        t = c * Qc + jnp.arange(Qc)
        kidx = t[:, None] - offs[None, :]
        ok = kidx >= 0
        kidx = jnp.maximum(kidx, 0)
        qc = lax.dynamic_slice_in_dim(q, c * Qc, Qc, axis=1)
        kg = k[:, kidx]
        vg = v[:, kidx]
        s = jnp.einsum('bthd,btnhd->bthn', qc, kg, preferred_element_type=jnp.float32) * scale
        p, lse = masked_softmax(s, ok[None, :, None, :])
        o = jnp.einsum('bthn,btnhd->bthd', p.astype(vg.dtype), vg)
        return o, lse

    o, lse = lax.map(chunk, jnp.arange(S // Qc))
    o = o.transpose(1, 0, 2, 3, 4).reshape(B, S, Hg, dh)
    lse = lse.transpose(1, 0, 2, 3).reshape(B, S, Hg)
    return o, lse


def dilated_mixer(q, k, v):
    outs, lses = [], []
    for g, (w, r) in enumerate(DIL_GROUPS):
        sl = slice(g * DIL_HEADS_PER_GROUP, (g + 1) * DIL_HEADS_PER_GROUP)
        o, l = dilated_group_attention(q[:, :, sl], k[:, :, sl], v[:, :, sl], w, r)
        outs.append(o)
        lses.append(l)
    alpha = jax.nn.softmax(jnp.stack(lses, axis=0), axis=0)
    o_all = jnp.stack(outs, axis=0)
    return jnp.sum(alpha[..., None].astype(o_all.dtype) * o_all, axis=0)


def moba_mixer(q, k, v):
    B, S, H, dh = q.shape
    L = MOBA_BLOCK
    nbk = -(-S // L)
    pad = nbk * L - S
    kb = jnp.pad(k, ((0, 0), (0, pad), (0, 0), (0, 0))).reshape(B, nbk, L, H, dh)
    vb = jnp.pad(v, ((0, 0), (0, pad), (0, 0), (0, 0))).reshape(B, nbk, L, H, dh)
    kmean = jnp.mean(kb.astype(jnp.float32), axis=2)
    gate = jnp.einsum('bthd,bjhd->bthj', q.astype(jnp.float32), kmean)
    t_pos = jnp.arange(S)
    past = jnp.arange(nbk)[None, :] < (t_pos // L)[:, None]
    gate = jnp.where(past[None, :, None, :], gate, NEG_INF)
    ktop = min(MOBA_TOPK, nbk)
    top_s, top_i = lax.top_k(gate, ktop)
    top_ok = top_s > NEG_INF * 0.5
    kbt = kb.transpose(0, 3, 1, 2, 4)
    vbt = vb.transpose(0, 3, 1, 2, 4)
    bi = jnp.arange(B)[:, None, None, None]
    hi = jnp.arange(H)[None, None, :, None]
    Qc = MOBA_Q_CHUNK
    nsel = ktop * L
    scale = dh ** -0.5

    def chunk(c):
        t0 = c * Qc
        tc = t0 + jnp.arange(Qc)
        qc = lax.dynamic_slice_in_dim(q, t0, Qc, axis=1)
        ic = lax.dynamic_slice_in_dim(top_i, t0, Qc, axis=1)
        okc = lax.dynamic_slice_in_dim(top_ok, t0, Qc, axis=1)
        kg = kbt[bi, hi, ic].reshape(B, Qc, H, nsel, dh)
        vg = vbt[bi, hi, ic].reshape(B, Qc, H, nsel, dh)
        s_sel = jnp.einsum('bthd,bthkd->bthk', qc, kg, preferred_element_type=jnp.float32) * scale
        m_sel = jnp.broadcast_to(okc[..., None], (B, Qc, H, ktop, L)).reshape(B, Qc, H, nsel)
        own = t0 // L
        k_own = lax.dynamic_index_in_dim(kb, own, axis=1, keepdims=False)
        v_own = lax.dynamic_index_in_dim(vb, own, axis=1, keepdims=False)
        s_own = jnp.einsum('bthd,blhd->bthl', qc, k_own, preferred_element_type=jnp.float32) * scale
        m_own = (own * L + jnp.arange(L))[None, :] <= tc[:, None]
        m_own = jnp.broadcast_to(m_own[None, :, None, :], (B, Qc, H, L))
        p, _ = masked_softmax(jnp.concatenate([s_sel, s_own], axis=-1),
                              jnp.concatenate([m_sel, m_own], axis=-1))
        p = p.astype(v.dtype)
        return (jnp.einsum('bthk,bthkd->bthd', p[..., :nsel], vg)
                + jnp.einsum('bthl,blhd->bthd', p[..., nsel:], v_own))

    o = lax.map(chunk, jnp.arange(S // Qc))
    return o.transpose(1, 0, 2, 3, 4).reshape(B, S, H, dh)


def hybrid_layer(x, cos, sin, attn_g, w_in, pe_k, w1_k, w2_k, pe_v, w1_v, w2_v,
                 w_br_a, w_br_b, w_br_c, w_o, mlp_g, w_mlp_in, w_mlp_out):
    B, S, D = x.shape
    h = rms_norm(x, attn_g)
    proj = jnp.einsum('bsd,dn->bsn', h, w_in)
    split_points = np.cumsum(IN_SPLIT_SIZES)[:-1].tolist()
    (a_q, a_kc, a_vc, a_ks, a_vs, a_kw, a_vw, a_g,
     b_q, b_k, b_v, c_q, c_k, c_v, m_a, m_b, m_c) = jnp.split(proj, split_points, axis=-1)

    def heads(t, n):
        return t.reshape(B, S, n, HEAD_DIM)

    a_gates = jax.nn.sigmoid(a_g.astype(jnp.float32)).reshape(B, S, NSA_HEADS, 3).astype(x.dtype)
    o_a = nsa_mixer(heads(a_q, NSA_HEADS),
                    heads(a_kc, NSA_KV_HEADS), heads(a_vc, NSA_KV_HEADS),
                    heads(a_ks, NSA_KV_HEADS), heads(a_vs, NSA_KV_HEADS),
                    heads(a_kw, NSA_KV_HEADS), heads(a_vw, NSA_KV_HEADS),
                    a_gates, pe_k, w1_k, w2_k, pe_v, w1_v, w2_v, cos, sin)
    o_b = dilated_mixer(apply_partial_rope(heads(b_q, DIL_HEADS), cos, sin),
                        apply_partial_rope(heads(b_k, DIL_HEADS), cos, sin),
                        heads(b_v, DIL_HEADS))
    o_c = moba_mixer(apply_partial_rope(heads(c_q, MOBA_HEADS), cos, sin),
                     apply_partial_rope(heads(c_k, MOBA_HEADS), cos, sin),
                     heads(c_v, MOBA_HEADS))

    y_a = jnp.einsum('bsk,kd->bsd', o_a.reshape(B, S, A_OUT), w_br_a)
    y_b = jnp.einsum('bsk,kd->bsd', o_b.reshape(B, S, B_OUT), w_br_b)
    y_c = jnp.einsum('bsk,kd->bsd', o_c.reshape(B, S, C_OUT), w_br_c)
    merged = jax.nn.sigmoid(m_a) * y_a + jax.nn.sigmoid(m_b) * y_b + jax.nn.sigmoid(m_c) * y_c
    x = x + jnp.einsum('bsd,de->bse', merged, w_o)

    h2 = rms_norm(x, mlp_g)
    u = jnp.square(jax.nn.relu(jnp.einsum('bsd,df->bsf', h2, w_mlp_in)))
    return x + jnp.einsum('bsf,fd->bsd', u, w_mlp_out)


def setup_inputs(seed: int = 0) -> dict:
    key = jax.random.key(seed)
    ks = jax.random.split(key, 18)
    f32 = jnp.float32
    D = D_MODEL

    def nrm(k, shape, scale):
        return jax.random.normal(k, shape, f32) * scale

    return {
        "x": nrm(ks[0], (BATCH, SEQ, D), 1.0),
        "attn_norm_g": 1.0 + nrm(ks[1], (DEPTH, D), 0.02),
        "w_in": nrm(ks[2], (DEPTH, D, IN_WIDTH), D ** -0.5),
        "cmp_pe_k": nrm(ks[3], (DEPTH, NSA_CMP_LEN, HEAD_DIM), 0.02),
        "cmp_w1_k": nrm(ks[4], (DEPTH, NSA_CMP_LEN, HEAD_DIM, NSA_CMP_HIDDEN), (NSA_CMP_LEN * HEAD_DIM) ** -0.5),
        "cmp_w2_k": nrm(ks[5], (DEPTH, NSA_CMP_HIDDEN, HEAD_DIM), NSA_CMP_HIDDEN ** -0.5),
        "cmp_pe_v": nrm(ks[6], (DEPTH, NSA_CMP_LEN, HEAD_DIM), 0.02),
        "cmp_w1_v": nrm(ks[7], (DEPTH, NSA_CMP_LEN, HEAD_DIM, NSA_CMP_HIDDEN), (NSA_CMP_LEN * HEAD_DIM) ** -0.5),
        "cmp_w2_v": nrm(ks[8], (DEPTH, NSA_CMP_HIDDEN, HEAD_DIM), NSA_CMP_HIDDEN ** -0.5),
        "w_br_a": nrm(ks[9], (DEPTH, A_OUT, D), A_OUT ** -0.5),
        "w_br_b": nrm(ks[10], (DEPTH, B_OUT, D), B_OUT ** -0.5),
        "w_br_c": nrm(ks[11], (DEPTH, C_OUT, D), C_OUT ** -0.5),
        "w_o": nrm(ks[12], (DEPTH, D, D), D ** -0.5),
        "mlp_norm_g": 1.0 + nrm(ks[13], (DEPTH, D), 0.02),
        "w_mlp_in": nrm(ks[14], (DEPTH, D, D_FF), D ** -0.5),
        "w_mlp_out": nrm(ks[15], (DEPTH, D_FF, D), D_FF ** -0.5),
        "final_norm_g": 1.0 + nrm(ks[16], (D,), 0.02),
    }


def reference(x, attn_norm_g, w_in, cmp_pe_k, cmp_w1_k, cmp_w2_k, cmp_pe_v, cmp_w1_v, cmp_w2_v,
              w_br_a, w_br_b, w_br_c, w_o, mlp_norm_g, w_mlp_in, w_mlp_out, final_norm_g):
    cos, sin = rope_tables(x.shape[1])
    for l in range(DEPTH):
        x = hybrid_layer(x, cos, sin, attn_norm_g[l], w_in[l],
                         cmp_pe_k[l], cmp_w1_k[l], cmp_w2_k[l], cmp_pe_v[l], cmp_w1_v[l], cmp_w2_v[l],
                         w_br_a[l], w_br_b[l], w_br_c[l], w_o[l],
                         mlp_norm_g[l], w_mlp_in[l], w_mlp_out[l])
    return rms_norm(x, final_norm_g)
```

```python
import contextlib
import math

import numpy as np

import concourse.bass as bass
import concourse.mybir as mybir
from concourse.bass_utils import run_bass_kernel_spmd

F32 = mybir.dt.float32
BF16 = mybir.dt.bfloat16
AF = mybir.ActivationFunctionType
ALU = mybir.AluOpType
AX = mybir.AxisListType

HD = 128
NEGB = -30000.0
BIG = 1.0e30


class Cfg:
    def __init__(self, D=2048, DFF=8192, S=2048, DEPTH=2):
        self.D, self.DFF, self.S, self.DEPTH = D, DFF, S, DEPTH
        self.KC = D // 128
        self.FC = DFF // 128
        self.NTB = S // 512
        self.NTT = S // 128
        o = 0
        self.col = {}
        for name, w in (("a_q", 1024), ("a_kc", 256), ("a_vc", 256), ("a_ks", 256), ("a_vs", 256),
                        ("a_kw", 256), ("a_vw", 256), ("a_g", 24), ("b_q", 1536), ("b_k", 1536),
                        ("b_v", 1536), ("c_q", 1024), ("c_k", 1024), ("c_v", 1024),
                        ("m_a", D), ("m_b", D), ("m_c", D)):
            self.col[name] = (o, w)
            o += w
        self.INW = o


_uid = [0]


class Sched:
    ENGS = ("pe", "act", "dve", "pool", "sp")
    NDMASEM = 8

    def __init__(self, nc, same_engine_sync=("act", "dve", "pool")):
        self.nc = nc
        self.ops = []
        self.last_writer = {}
        self.readers = {}
        self.same_engine_sync = set(same_engine_sync)

    def _add(self, eng, fn, reads, writes, is_dma):
        idx = len(self.ops)
        deps = set()
        for r in reads:
            w = self.last_writer.get(r)
            if w is not None:
                deps.add(w)
        for w_ in writes:
            w = self.last_writer.get(w_)
            if w is not None:
                deps.add(w)
            for r in self.readers.get(w_, ()):
                deps.add(r)
        deps.discard(idx)
        self.ops.append(dict(eng=eng, fn=fn, deps=deps, dma=is_dma))
        for r in reads:
            self.readers.setdefault(r, []).append(idx)
        for w_ in writes:
            self.last_writer[w_] = idx
            self.readers[w_] = []
        return idx

    def op(self, eng, fn, reads=(), writes=()):
        return self._add(eng, fn, tuple(reads), tuple(writes), False)

    def dma(self, eng, out, in_, reads=(), writes=(), slow=False):
        if slow:
            return self._add(eng, lambda e: e.dma_start(out=out, in_=in_, allow_slow_non_contiguous=True),
                             tuple(reads), tuple(writes), True)
        return self._add(eng, lambda e: e.dma_start(out=out, in_=in_), tuple(reads), tuple(writes), True)

    def run(self, stack):
        nc = self.nc
        ops = self.ops
        if not ops:
            return
        needed = [False] * len(ops)
        for i, o in enumerate(ops):
            if o["dma"]:
                needed[i] = True
            keep = set()
            for d in o["deps"]:
                od = ops[d]
                if od["dma"] or o["dma"] or od["eng"] != o["eng"]:
                    keep.add(d)
                elif o["eng"] in self.same_engine_sync:
                    keep.add(d)
            o["deps"] = keep
            for d in keep:
                needed[d] = True
        with nc.cleanup_on_exit():
            self._emit(nc, ops, needed)

    def _emit(self, nc, ops, needed):
        _uid[0] += 1
        u = _uid[0]
        sems = {e: nc.alloc_semaphore(name="s%d_%s" % (u, e)) for e in self.ENGS}
        dsems = {e: [nc.alloc_semaphore(name="d%d_%s%d" % (u, e, k)) for k in range(self.NDMASEM)]
                 for e in ("act", "pool", "sp")}
        cnt = {e: 0 for e in self.ENGS}
        dcnt = {e: 0 for e in dsems}
        duse = {}
        final = {}
        for i, o in enumerate(ops):
            if o["dma"]:
                e = o["eng"]
                s = dsems[e][dcnt[e] % self.NDMASEM]
                dcnt[e] += 1
                duse[s.name] = duse.get(s.name, 0) + 1
                o["sig"] = (s, 16 * duse[s.name])
                o["prev"] = (s, 16 * (duse[s.name] - 1)) if duse[s.name] > 1 else None
                final[s.name] = o["sig"]
            elif needed[i]:
                cnt[o["eng"]] += 1
                o["sig"] = (sems[o["eng"]], cnt[o["eng"]])
        per = {e: [i for i, o in enumerate(ops) if o["eng"] == e] for e in self.ENGS}

        def emit(e, engobj):
            waited = {}

            def wait(s, v):
                if waited.get(s.name, 0) < v:
                    engobj.wait_ge(s, v)
                    waited[s.name] = v

            for i in per[e]:
                o = ops[i]
                for d in sorted(o["deps"]):
                    wait(*ops[d]["sig"])
                if o["dma"] and o["prev"] is not None:
                    wait(*o["prev"])
                ins = o["fn"](engobj)
                if "sig" in o:
                    ins.then_inc(o["sig"][0], 16 if o["dma"] else 1)
            if e == "sp":
                for s, v in final.values():
                    wait(s, v)

        with nc.Block() as block:
            block.sync(lambda en: emit("sp", en))
            if per["pe"]:
                block.tensor(lambda en: emit("pe", en))
            if per["act"]:
                block.scalar(lambda en: emit("act", en))
            if per["dve"]:
                block.vector(lambda en: emit("dve", en))
            if per["pool"]:
                block.gpsimd(lambda en: emit("pool", en))


class Phase:
    def __init__(self, nc):
        self.nc = nc
        self.st = contextlib.ExitStack()
        self.S = Sched(nc)

    def T(self, name, shape, dt):
        _uid[0] += 1
        return self.st.enter_context(self.nc.sbuf_tensor("%s_%d" % (name, _uid[0]), list(shape), dt))

    def P(self, name, shape, dt):
        _uid[0] += 1
        return self.st.enter_context(self.nc.psum_tensor("%s_%d" % (name, _uid[0]), list(shape), dt))

    def run(self):
        self.S.run(self.st)
        self.st.close()


MASK_KEYS = []


def _mask_keys():
    keys = []
    for off in (0, -128, -256, -384):
        keys.append((off, None, 1))
    for off in (128, 256, 384, 512):
        keys.append((off, 512, 1))
    for off in (-384, -256, -128, 0, 128):
        keys.append((off, 129, 1))
    for off in (-384, -256, -128, 0, 128, 256, 384, 512):
        keys.append((off, 513, 4))
    for off in (-384, -256, -128, 0, 4096):
        keys.append((off, None, 16))
    return keys


def build_consts(nc, cfg, C, st, which):
    S_ = cfg.S

    def T(name, shape, dt):
        _uid[0] += 1
        return st.enter_context(nc.sbuf_tensor("%s_%d" % (name, _uid[0]), list(shape), dt))

    ph = Phase(nc)
    S = ph.S
    pi = math.pi

    def ts(eng, out, in0, s1, s2, op0, op1=None, r=(), w=()):
        op_ts(S, eng, out, in0, s1, s2, op0, op1, r=r, w=w)

    def tt(eng, out, in0, in1, op, r=(), w=()):
        op_tt(S, eng, out, in0, in1, op, r=r, w=w)

    def iota(out, pattern, base, cm, w):
        S.op("pool", lambda e: e.iota(out, pattern=pattern, base=base, channel_multiplier=cm,
                                      allow_small_or_imprecise_dtypes=True), writes=w)

    if which == "global":
        C["ident"] = T("c_ident", [128, 128], BF16)
        C["ones"] = T("c_ones", [128, 128], BF16)
        C["pm"] = T("c_pm", [32, 32], BF16)
        C["eps"] = T("c_eps", [128, 1], F32)
    if which == "rope":
        C["ropeC"] = T("c_ropeC", [32, S_], F32)
        C["ropeS"] = T("c_ropeS", [32, S_], F32)
    if which == "attn":
        keys = _mask_keys()
        C["mask_keys"] = {k: i for i, k in enumerate(keys)}
        C["masks"] = T("c_masks", [128, len(keys), 512], BF16)
        C["cmask"] = T("c_cmask", [128, S_], BF16)
        C["E32"] = T("c_E32", [32, S_], BF16)
        C["E8"] = T("c_E8", [8, S_], BF16)
        C["gsel"] = T("c_gsel", [24, 24, 128], BF16)
        C["ovx"] = T("c_ovx", [128, 33], BF16)
        C["forced"] = T("c_forced", [128, 8, 32], F32)
        C["cand"] = T("c_cand", [128, 8, 32], F32)
        C["negbig"] = T("c_negbig", [128, 8, 32], F32)
        C["pastneg"] = T("c_pastneg", [128, 16, 8], F32)
        C["own"] = T("c_own", [128, 16, 8], F32)

    w1 = ph.T("w1", [128, 512], F32)
    w2 = ph.T("w2", [128, 512], F32)
    w3 = ph.T("w3", [128, 512], F32)
    col = ph.T("col", [128, 8], F32)
    coli = ph.T("coli", [128, 2], mybir.dt.int32)
    iota(coli[:, 0:1], [[0, 1]], 0, 1, ["coli0"])
    S.op("dve", lambda e: e.tensor_copy(out=col[:, 0:1], in_=coli[:, 0:1]), reads=["coli0"], writes=["col0"])

    if which == "global":
        iota(w1[:, 0:128], [[1, 128]], 0, -1, ["w1"])
        ts("dve", C["ident"][:], w1[:, 0:128], 0.0, None, ALU.is_equal, r=["w1"], w=["ident"])
        ts("dve", w2[0:32, 0:32], w1[0:32, 0:32], 16.0, None, ALU.is_equal, r=["w1"], w=["w2"])
        ts("dve", w3[0:32, 0:32], w1[0:32, 0:32], -16.0, None, ALU.is_equal, r=["w1"], w=["w3"])
        tt("dve", C["pm"][:], w2[0:32, 0:32], w3[0:32, 0:32], ALU.max, r=["w2", "w3"], w=["pm"])
        S.op("dve", lambda e: e.memset(C["ones"][:], 1.0), writes=["ones"])
        S.op("dve", lambda e: e.memset(C["eps"][:], 1e-30), writes=["eps"])

    if which == "rope":
        big = ph.T("big", [32, S_], F32)
        big2 = ph.T("big2", [32, S_], F32)
        kf = ph.T("kf", [32, S_], F32)
        ki = ph.T("ki", [32, S_], mybir.dt.int32)
        op_ts(S, "dve", coli[:, 1:2], coli[:, 0:1], 15, None, ALU.bitwise_and, r=["coli0"], w=["coli1"])
        S.op("dve", lambda e: e.tensor_copy(out=col[:, 1:2], in_=coli[:, 1:2]), reads=["coli1"], writes=["col1"])
        ts("dve", col[:, 1:2], col[:, 1:2], -1.0 / 16.0, None, ALU.mult, r=["col1"], w=["col1"])
        S.op("dve", lambda e: e.memset(col[:, 5:6], 500000.0), writes=["col5"])
        tt("pool", col[:, 2:3], col[:, 5:6], col[:, 1:2], ALU.pow, r=["col1", "col5"], w=["col2"])
        ts("dve", col[:, 3:4], col[:, 0:1], 16.0, 2.0, ALU.is_ge, ALU.mult, r=["col0"], w=["col3"])
        ts("dve", col[:, 3:4], col[:, 3:4], -1.0, None, ALU.add, r=["col3"], w=["col3"])
        iota(big[:], [[1, S_]], 0, 0, ["big"])
        ts("dve", big2[:], big[:], col[0:32, 2:3], None, ALU.mult, r=["big", "col2"], w=["big2"])

        def sin_of(dst, shift, sgn_col):
            ts("dve", big[:], big2[:], shift, None, ALU.add, r=["big2"], w=["big"])
            ts("dve", kf[:], big[:], 1.0 / (2 * pi), None, ALU.mult, r=["big"], w=["kf"])
            S.op("dve", lambda e: e.tensor_copy(out=ki[:], in_=kf[:]), reads=["kf"], writes=["ki"])
            S.op("dve", lambda e: e.tensor_copy(out=kf[:], in_=ki[:]), reads=["ki"], writes=["kf"])
            S.op("dve", lambda e: e.scalar_tensor_tensor(out=big[:], in0=kf[:], scalar=-2 * pi, in1=big[:], op0=ALU.mult, op1=ALU.add),
                 reads=["kf", "big"], writes=["big"])
            ts("dve", kf[:], big[:], pi, None, ALU.is_gt, r=["big"], w=["kf"])
            S.op("dve", lambda e: e.scalar_tensor_tensor(out=big[:], in0=kf[:], scalar=-2 * pi, in1=big[:], op0=ALU.mult, op1=ALU.add),
                 reads=["kf", "big"], writes=["big"])
            ts("dve", kf[:], big[:], -pi, None, ALU.is_lt, r=["big"], w=["kf"])
            S.op("dve", lambda e: e.scalar_tensor_tensor(out=big[:], in0=kf[:], scalar=2 * pi, in1=big[:], op0=ALU.mult, op1=ALU.add),
                 reads=["kf", "big"], writes=["big"])
            ts("dve", big[:], big[:], pi, -pi, ALU.min, ALU.max, r=["big"], w=["big"])
            if sgn_col is None:
                S.op("act", lambda e: e.activation(out=dst, in_=big[:], func=AF.Sin), reads=["big"], writes=["ropeC"])
            else:
                S.op("act", lambda e: e.activation(out=big[:], in_=big[:], func=AF.Sin), reads=["big"], writes=["big"])
                ts("dve", dst, big[:], sgn_col, None, ALU.mult, r=["big", "col3"], w=["ropeS"])

        sin_of(C["ropeS"][:], 0.0, col[0:32, 3:4])
        sin_of(C["ropeC"][:], 0.5 * pi, None)

    if which == "attn":
        big = ph.T("big", [128, S_], F32)
        big2 = ph.T("big2", [128, S_], F32)
        w1i = ph.T("w1i", [128, 512], mybir.dt.int32)
        w3i = ph.T("w3i", [128, 512], mybir.dt.int32)
        ts("dve", col[:, 4:5], col[:, 0:1], 64.0, None, ALU.is_ge, r=["col0"], w=["col4"])
        for k, (off, W, dil) in enumerate(keys):
            iota(w1[:], [[1, 512]], off, -1, ["w1"])
            ts("dve", w2[:], w1[:], 0.0, None, ALU.is_ge, r=["w1"], w=["w2"])
            if W is not None:
                ts("dve", w3[:], w1[:], float(W - 1), None, ALU.is_le, r=["w1"], w=["w3"])
                tt("dve", w2[:], w2[:], w3[:], ALU.mult, r=["w2", "w3"], w=["w2"])
            if dil > 1:
                iota(w1i[:], [[1, 512]], off, -1, ["w1i"])
                op_ts(S, "dve", w3i[:], w1i[:], dil - 1, None, ALU.bitwise_and, r=["w1i"], w=["w3i"])
                op_ts(S, "dve", w3[:], w3i[:], 0, None, ALU.is_equal, r=["w3i"], w=["w3"])
                tt("dve", w2[:], w2[:], w3[:], ALU.mult, r=["w2", "w3"], w=["w2"])
            ts("dve", C["masks"][:, k, :], w2[:], -1.0, -NEGB, ALU.add, ALU.mult, r=["w2"], w=["mask%d" % k])
        iota(big[:], [[1, S_]], -31, -16, ["big"])
        ts("dve", big[:], big[:], 0.0, None, ALU.is_ge, r=["big"], w=["big"])
        ts("dve", C["cmask"][:], big[:], -1.0, -NEGB, ALU.add, ALU.mult, r=["big"], w=["cmask"])
        iota(big2[0:32, :], [[1, S_]], 0, -64, ["big2"])
        ts("dve", big[0:32, :], big2[0:32, :], 0.0, None, ALU.is_ge, r=["big2"], w=["big"])
        ts("dve", big2[0:32, :], big2[0:32, :], 63.0, None, ALU.is_le, r=["big2"], w=["big2"])
        tt("dve", C["E32"][:], big[0:32, :], big2[0:32, :], ALU.mult, r=["big", "big2"], w=["E32"])
        iota(big2[0:8, :], [[1, S_]], 0, -256, ["big2"])
        ts("dve", big[0:8, :], big2[0:8, :], 0.0, None, ALU.is_ge, r=["big2"], w=["big"])
        ts("dve", big2[0:8, :], big2[0:8, :], 255.0, None, ALU.is_le, r=["big2"], w=["big2"])
        tt("dve", C["E8"][:], big[0:8, :], big2[0:8, :], ALU.mult, r=["big", "big2"], w=["E8"])
        g1 = ph.T("g1", [24, 24, 128], F32)
        iota(g1[:], [[1, 24], [0, 128]], 0, -1, ["g1"])
        ts("dve", C["gsel"][:], g1[:], 0.0, None, ALU.is_equal, r=["g1"], w=["gsel"])
        iota(w1[:, 0:32], [[-64, 32]], 0, 16, ["w1"])
        ts("dve", w2[:, 0:32], w1[:, 0:32], 32.0, 64.0, ALU.add, ALU.min, r=["w1"], w=["w2"])
        ts("dve", w3[:, 0:32], w1[:, 0:32], 0.0, None, ALU.max, r=["w1"], w=["w3"])
        tt("dve", w2[:, 0:32], w2[:, 0:32], w3[:, 0:32], ALU.subtract, r=["w2", "w3"], w=["w2"])
        ts("dve", w2[:, 0:32], w2[:, 0:32], 0.0, 1.0 / 32.0, ALU.max, ALU.mult, r=["w2"], w=["w2"])
        S.op("dve", lambda e: e.memset(w2[:, 32:33], 1.0), reads=["w2"], writes=["w2"])
        S.op("dve", lambda e: e.tensor_copy(out=C["ovx"][:], in_=w2[:, 0:33]), reads=["w2"], writes=["ovx"])
        q1 = ph.T("q1", [128, 8, 32], F32)
        q2 = ph.T("q2", [128, 8, 32], F32)
        q3 = ph.T("q3", [128, 8, 32], F32)
        iota(q1[:], [[2, 8], [-1, 32]], 16, 0, ["q1"])
        ts("dve", q1[:], q1[:], col[:, 4:5], None, ALU.add, r=["q1", "col4"], w=["q1"])
        ts("dve", C["cand"][:], q1[:], 2.0, None, ALU.is_ge, r=["q1"], w=["cand"])
        ts("dve", q2[:], q1[:], 0.0, None, ALU.is_ge, r=["q1"], w=["q2"])
        ts("dve", q3[:], q1[:], 1.0, None, ALU.is_le, r=["q1"], w=["q3"])
        tt("dve", C["forced"][:], q2[:], q3[:], ALU.mult, r=["q2", "q3"], w=["forced"])
        S.op("dve", lambda e: e.memset(C["forced"][:, :, 0:1], 1.0), reads=["forced"], writes=["forced"])
        S.op("dve", lambda e: e.memset(C["cand"][:, :, 0:1], 0.0), reads=["cand"], writes=["cand"])
        ts("dve", C["negbig"][:], C["cand"][:], -1.0, BIG, ALU.add, ALU.mult, r=["cand"], w=["negbig"])
        m1 = ph.T("m1", [128, 16, 8], F32)
        m2 = ph.T("m2", [128, 16, 8], F32)
        iota(m1[:], [[-1, 16], [2, 8]], 0, 0, ["m1"])
        ts("dve", C["pastneg"][:], m1[:], -1.5, -BIG, ALU.is_ge, ALU.mult, r=["m1"], w=["pastneg"])
        ts("dve", m2[:], m1[:], -1.0, None, ALU.is_ge, r=["m1"], w=["m2"])
        ts("dve", C["own"][:], m1[:], 0.0, None, ALU.is_le, r=["m1"], w=["own"])
        tt("dve", C["own"][:], C["own"][:], m2[:], ALU.mult, r=["own", "m2"], w=["own"])
    ph.run()


def mask_ap(C, off, W, dil):
    return C["masks"][:, C["mask_keys"][(off, W, dil)], :]


def op_ts(S, eng, out, in0, s1, s2, op0, op1=None, r=(), w=()):
    if op1 is None:
        S.op(eng, lambda e: e.tensor_scalar(out=out, in0=in0, scalar1=s1, scalar2=None, op0=op0), reads=r, writes=w)
    else:
        S.op(eng, lambda e: e.tensor_scalar(out=out, in0=in0, scalar1=s1, scalar2=s2, op0=op0, op1=op1), reads=r, writes=w)


def op_tt(S, eng, out, in0, in1, op, r=(), w=()):
    S.op(eng, lambda e: e.tensor_tensor(out=out, in0=in0, in1=in1, op=op), reads=r, writes=w)


def op_act(S, out, in_, func, r=(), w=(), **kw):
    S.op("act", lambda e: e.activation(out=out, in_=in_, func=func, **kw), reads=r, writes=w)


def op_mm(S, out, lhsT, rhs, start, stop, r=(), w=()):
    S.op("pe", lambda e: e.matmul(out, lhsT, rhs, start=start, stop=stop), reads=r, writes=w)


def phase_norm(nc, cfg, C, x_src, g_ap, hT, ntt):
    D, KC = cfg.D, cfg.KC
    ph = Phase(nc)
    S = ph.S
    cpb = min(8, KC)
    nbk = KC // cpb
    gT = ph.T("gT", [128, KC], F32)
    S.dma("sp", gT[:], g_ap.rearrange("(k p) -> p k", p=128), writes=["gT"], slow=True)
    NB = 3 if 3 * nbk <= 8 else 2
    xt = [ph.T("xt%d" % i, [128, D], F32) for i in range(NB)]
    xn = [ph.T("xn%d" % i, [128, D], BF16) for i in range(NB)]
    junk = ph.T("junk", [128, D], BF16)
    st = ph.T("st", [128, ntt, 4], F32)
    ptr = [ph.P("ptr%d" % i, [128, cpb * 128], BF16) for i in range(NB * nbk)]
    S.op("dve", lambda e: e.memset(st[:], 0.0), writes=["st"])
    for tt in range(ntt):
        b = tt % NB
        xb, nb = "xt%d" % b, "xn%d" % b
        S.dma("sp", xt[b][:], x_src[tt * 128:(tt + 1) * 128, :], writes=[xb])
        op_act(S, junk[:], xt[b][:], AF.Square, r=[xb, "st"], w=["junk", "s0_%d" % tt], accum_out=st[:, tt, 0:1])
        op_ts(S, "dve", st[:, tt, 1:2], st[:, tt, 0:1], 1.0 / D, 1e-6, ALU.mult, ALU.add, r=["s0_%d" % tt], w=["s1_%d" % tt])
        op_act(S, st[:, tt, 2:3], st[:, tt, 1:2], AF.Sqrt, r=["s1_%d" % tt], w=["s2_%d" % tt])
        S.op("dve", lambda e, tt=tt: e.reciprocal(out=st[:, tt, 3:4], in_=st[:, tt, 2:3]), reads=["s2_%d" % tt], writes=["s3_%d" % tt])
        op_ts(S, "dve", xn[b][:], xt[b][:], st[:, tt, 3:4], None, ALU.mult, r=[xb, "s3_%d" % tt], w=[nb])
        for k in range(nbk):
            pt = ptr[b * nbk + k]
            pn = "ptr%d" % (b * nbk + k)
            for c in range(cpb):
                kc = k * cpb + c
                S.op("pe", lambda e, pt=pt, c=c, kc=kc, b=b: e.transpose(pt[:, c * 128:(c + 1) * 128], xn[b][:, kc * 128:(kc + 1) * 128], C["ident"][:]),
                     reads=[nb], writes=[pn])
            op_tt(S, "dve", hT[:, k * cpb:(k + 1) * cpb, tt * 128:(tt + 1) * 128],
                  pt[:, :].rearrange("p (k t) -> p k t", t=128),
                  gT[:, k * cpb:(k + 1) * cpb].unsqueeze(2).to_broadcast([128, cpb, 128]), ALU.mult,
                  r=[pn, "gT"], w=["hT"])
    ph.run()


def fm_index(cfg):
    idx = {}
    n = 0
    for name, cnt in (("a_q_raw", 8), ("a_q", 8), ("a_kc", 2), ("a_vc", 2), ("a_ks", 2), ("a_kw", 2),
                      ("b_q", 12), ("b_k", 12), ("c_q", 8), ("c_k", 8),
                      ("m_a", cfg.KC), ("m_b", cfg.KC), ("m_c", cfg.KC)):
        idx[name] = n
        n += cnt
    idx["_n"] = n
    return idx


def p1_spans(cfg):
    spans = []
    fi = fm_index(cfg)

    def fm(name, kind):
        c0, w = cfg.col[name]
        nch = w // 128
        for s in range(0, nch, 4):
            k = min(4, nch - s)
            spans.append(dict(c0=c0 + s * 128, w=k * 128, kind=kind, base=fi[name] + s,
                              rawbase=(fi["a_q_raw"] + s) if kind == "both" else None))

    def tm(name, dst, dcol):
        c0, w = cfg.col[name]
        for s in range(0, w, 512):
            k = min(512, w - s)
            spans.append(dict(c0=c0 + s, w=k, kind="tm", dst=dst, dcol=dcol + s))

    fm("a_q", "both")
    fm("a_kc", "raw")
    fm("a_vc", "raw")
    fm("a_ks", "rot")
    tm("a_vs", "vA", 0)
    fm("a_kw", "rot")
    tm("a_vw", "vA", 256)
    spans.append(dict(c0=cfg.col["a_g"][0], w=24, kind="gate"))
    fm("b_q", "rot")
    fm("b_k", "rot")
    tm("b_v", "vB", 0)
    fm("c_q", "rot")
    fm("c_k", "rot")
    tm("c_v", "vC", 0)
    fm("m_a", "sig")
    fm("m_b", "sig")
    fm("m_c", "sig")
    return spans


def phase_inproj(nc, cfg, C, Dm, l, hT, SL, tok0):
    KC = cfg.KC
    ntb, ntt = SL // 512, SL // 128
    ph = Phase(nc)
    S = ph.S
    w_in = Dm["w_in"]
    NW = 3
    wb = [ph.T("wb%d" % i, [128, KC, 512], BF16) for i in range(NW)]
    stg = [ph.T("stg%d" % i, [128, SL], BF16) for i in range(4)]
    vst = [ph.T("vst%d" % i, [128, 512], BF16) for i in range(3)]
    qtmp = [ph.T("qtmp%d" % i, [32, 512], BF16) for i in range(2)]
    t1 = [ph.T("t1_%d" % i, [32, 512], F32) for i in range(2)]
    t2 = [ph.T("t2_%d" % i, [32, 512], F32) for i in range(2)]
    psm = [ph.P("psm%d" % i, [128, 512], F32) for i in range(4)]
    psw = [ph.P("psw%d" % i, [128, 512], F32) for i in range(2)]
    cnt = dict(ps=0, stg=0, vst=0, rot=0)

    def new_stg():
        i = cnt["stg"] % 4
        cnt["stg"] += 1
        return stg[i], "stg%d" % i

    for si, sp in enumerate(p1_spans(cfg)):
        wi = si % NW
        wt, wn = wb[wi], "wb%d" % wi
        c0, w = sp["c0"], sp["w"]
        S.dma("pool", wt[:, :, 0:w], w_in[l, :, c0:c0 + w].rearrange("(k p) n -> p k n", p=128), writes=[wn])
        kind = sp["kind"]
        if kind == "tm":
            dst = Dm[sp["dst"]]
            for tt in range(ntt):
                pi_ = cnt["ps"] % 4
                cnt["ps"] += 1
                ps, pn = psm[pi_], "psm%d" % pi_
                for kc in range(KC):
                    op_mm(S, ps[:, 0:w], hT[:, kc, tt * 128:(tt + 1) * 128], wt[:, kc, 0:w], kc == 0, kc == KC - 1, r=[wn], w=[pn])
                vi = cnt["vst"] % 3
                cnt["vst"] += 1
                op_act(S, vst[vi][:, 0:w], ps[:, 0:w], AF.Copy, r=[pn], w=["vst%d" % vi])
                S.dma("sp", dst[tok0 + tt * 128:tok0 + (tt + 1) * 128, sp["dcol"]:sp["dcol"] + w], vst[vi][:, 0:w], reads=["vst%d" % vi])
            continue
        if kind == "gate":
            sg, sn = new_stg()
            for tb in range(ntb):
                pi_ = cnt["ps"] % 4
                cnt["ps"] += 1
                ps, pn = psm[pi_], "psm%d" % pi_
                for kc in range(KC):
                    op_mm(S, ps[0:24, :], wt[:, kc, 0:24], hT[:, kc, tb * 512:(tb + 1) * 512], kc == 0, kc == KC - 1, r=[wn], w=[pn])
                op_act(S, sg[0:24, tb * 512:(tb + 1) * 512], ps[0:24, :], AF.Sigmoid, r=[pn], w=[sn + "_%d" % tb])
            S.dma("sp", Dm["agT"][:, tok0:tok0 + SL], sg[0:24, :], reads=[sn + "_%d" % tb for tb in range(ntb)] + [sn], writes=[sn])
            continue
        for ci in range(w // 128):
            sg, sn = new_stg()
            if kind == "both":
                sr, srn = new_stg()
            for tb in range(ntb):
                ts_ = slice(tb * 512, (tb + 1) * 512)
                pi_ = cnt["ps"] % 4
                cnt["ps"] += 1
                ps, pn = psm[pi_], "psm%d" % pi_
                for kc in range(KC):
                    op_mm(S, ps[:, :], wt[:, kc, ci * 128:(ci + 1) * 128], hT[:, kc, ts_], kc == 0, kc == KC - 1, r=[wn], w=[pn])
                if kind == "raw":
                    op_act(S, sg[:, ts_], ps[:, :], AF.Copy, r=[pn], w=[sn + "_%d" % tb])
                elif kind == "sig":
                    op_act(S, sg[:, ts_], ps[:, :], AF.Sigmoid, r=[pn], w=[sn + "_%d" % tb])
                else:
                    ri = cnt["rot"] % 2
                    cnt["rot"] += 1
                    qn, pwn = "qtmp%d" % ri, "psw%d" % ri
                    op_act(S, qtmp[ri][:, :], ps[0:32, :], AF.Copy, r=[pn], w=[qn])
                    if kind == "both":
                        op_act(S, sr[:, ts_], ps[:, :], AF.Copy, r=[pn], w=[srn + "_%d" % tb])
                    op_act(S, sg[:, ts_], ps[:, :], AF.Copy, r=[pn], w=[sn + "_%d" % tb])
                    op_mm(S, psw[ri][0:32, :], C["pm"][:, :], qtmp[ri][:, :], True, True, r=[qn], w=[pwn])
                    gs = slice(tok0 + tb * 512, tok0 + (tb + 1) * 512)
                    op_tt(S, "dve", t1[ri][:, :], psw[ri][0:32, :], C["ropeS"][:, gs], ALU.mult, r=[pwn], w=["t1_%d" % ri])
                    op_tt(S, "dve", t2[ri][:, :], ps[0:32, :], C["ropeC"][:, gs], ALU.mult, r=[pn], w=["t2_%d" % ri])
                    op_tt(S, "dve", sg[0:32, ts_], t1[ri][:, :], t2[ri][:, :], ALU.add, r=["t1_%d" % ri, "t2_%d" % ri], w=[sn + "_%d" % tb])
            allr = [sn] + [sn + "_%d" % tb for tb in range(ntb)] + [sn + "h_%d" % tb for tb in range(ntb)]
            S.dma("sp", Dm["projT"][sp["base"] + ci, :, tok0:tok0 + SL], sg[:, :], reads=allr, writes=allr)
            if kind == "both":
                allr = [srn] + [srn + "_%d" % tb for tb in range(ntb)]
                S.dma("sp", Dm["projT"][sp["rawbase"] + ci, :, tok0:tok0 + SL], sr[:, :], reads=allr, writes=allr)
    ph.run()


class Attn:
    LOOK = 2

    def __init__(self, ph, C):
        self.ph, self.C, self.S = ph, C, ph.S
        self.ps_s = [ph.P("pss%d" % i, [128, 512], F32) for i in range(3)]
        self.ps_o = [ph.P("pso%d" % i, [128, 512], F32) for i in range(2)]
        self.ps_d = [ph.P("psd%d" % i, [128, 512], F32) for i in range(2)]
        self.eb = [ph.T("eb%d" % i, [128, 512], BF16) for i in range(4)]
        self.rec = [ph.T("rec%d" % i, [128, 512], F32) for i in range(2)]
        self.ns = 0
        self.ne = 0
        self.nb = 0
        self.nm = 0
        self.queue = []

    def block(self, tiles, scale, done):
        S = self.S
        ob = self.nb % 2
        self.nb += 1
        n = len(tiles)
        for k, t in enumerate(tiles):
            nk = t["nk"]
            si = self.ns % 3
            self.ns += 1
            pss, pssn = self.ps_s[si], "pss%d" % si
            parts = [(t["k"], t["q"], list(t["reads"]))]
            if t.get("bias") is not None:
                parts.append((t["bias"][0], t["bias"][1], list(t["bias"][2])))
            if t.get("mask") is not None:
                parts.append((self.C["ident"][0:nk, 0:nk], t["mask"], []))
            for pi_, (lh, rh, rd) in enumerate(parts):
                op_mm(S, pss[0:nk, :], lh, rh, pi_ == 0, pi_ == len(parts) - 1, r=rd, w=[pssn])
            ei = self.ne % 4
            self.ne += 1
            eb, ebn = self.eb[ei], "eb%d" % ei
            op_act(S, eb[0:nk, :], pss[0:nk, :], AF.Exp, r=[pssn], w=[ebn], scale=scale)
            if t.get("extra") is not None:
                t["extra"](eb, ebn)
            self.queue.append((t, eb, ebn, k, n, ob, done))
            if len(self.queue) > self.LOOK:
                self._pop()

    def _pop(self):
        S, C = self.S, self.C
        t, eb, ebn, k, n, ob, done = self.queue.pop(0)
        pso, pson = self.ps_o[ob], "pso%d" % ob
        psd, psdn = self.ps_d[ob], "psd%d" % ob
        nk = t["nk"]
        op_mm(S, pso[:, :], t["v"], eb[0:nk, :], k == 0, k == n - 1, r=[ebn] + list(t["reads"]), w=[pson])
        op_mm(S, psd[:, :], C["ones"][0:nk, :], eb[0:nk, :], k == 0, k == n - 1, r=[ebn], w=[psdn])
        if k == n - 1:
            rec, recn = self.rec[ob], "rec%d" % ob
            op_act(S, rec[:, :], psd[:, :], AF.Ln, r=[psdn], w=[recn], bias=C["eps"][:, 0:1])
            op_act(S, rec[:, :], rec[:, :], AF.Exp, r=[recn], w=[recn], scale=-1.0)
            done(pso, pson, rec, recn)

    def flush(self):
        while self.queue:
            self._pop()


def load_fm(S, eng, dst, Dm, idx, w):
    S.dma(eng, dst, Dm["projT"][idx, :, :], writes=w)


def load_v(S, eng, dst, vten, col0, w):
    S.dma(eng, dst, vten[:, col0:col0 + 128].rearrange("(j p) c -> p j c", p=128), writes=w)


def phase_nsa_compress(nc, cfg, C, Dm, l, groups=(0, 1)):
    S_ = cfg.S
    fi = fm_index(cfg)
    ph = Phase(nc)
    S = ph.S
    psg = ph.P("psg", [128, 512], F32)
    psx = ph.P("psx", [128, 512], F32)
    w1 = {}
    w2 = {}
    pe = {}
    for kv in ("k", "v"):
        w1[kv] = ph.T("w1" + kv, [128, 32, 256], BF16)
        w2[kv] = ph.T("w2" + kv, [128, 2, 128], BF16)
        pe[kv] = ph.T("pe" + kv, [128, 32], BF16)
        S.dma("pool", w1[kv][:, :, :], Dm["cmp_w1_" + kv][l].rearrange("l d f -> d l f"), writes=["w1" + kv])
        S.dma("pool", w2[kv][:, :, :], Dm["cmp_w2_" + kv][l].rearrange("(c p) d -> p c d", p=128), writes=["w2" + kv])
        S.dma("pool", pe[kv][:, :], Dm["cmp_pe_" + kv][l].rearrange("l d -> d l"), writes=["pe" + kv], slow=True)
    raw = {kv: ph.T("raw" + kv, [128, S_], BF16) for kv in ("k", "v")}
    hid = {kv: ph.T("hid" + kv, [128, 2, 128], BF16) for kv in ("k", "v")}
    b1 = ph.T("b1", [128, 4], F32)
    zt = ph.T("zt", [128, 128], F32)
    ut = ph.T("ut", [128, 128], F32)
    sgt = ph.T("sgt", [128, 128], F32)
    kcmp = ph.T("kcmp", [128, 128], BF16)
    vcmp = ph.T("vcmp", [128, 128], BF16)
    S.op("dve", lambda e: e.memset(kcmp[:, :], 0.0), writes=["kcmp"])
    S.op("dve", lambda e: e.memset(vcmp[:, :], 0.0), writes=["vcmp"])
    for g in groups:
        load_fm(S, "sp", raw["k"][:, :], Dm, fi["a_kc"] + g, ["rawk"])
        load_fm(S, "sp", raw["v"][:, :], Dm, fi["a_vc"] + g, ["rawv"])
        for kv in ("k", "v"):
            raw3 = raw[kv][:, :].rearrange("p (m s) -> p m s", s=16)
            for fc in range(2):
                for li in range(32):
                    op_mm(S, psx[:, 0:127], w1[kv][:, li, fc * 128:(fc + 1) * 128], raw3[:, li // 16:li // 16 + 127, li % 16],
                          li == 0, li == 31, r=["w1" + kv, "raw" + kv], w=["psx"])
                for li in range(32):
                    op_mm(S, psg[:, 0:1], w1[kv][:, li, fc * 128:(fc + 1) * 128], pe[kv][:, li:li + 1],
                          li == 0, li == 31, r=["w1" + kv, "pe" + kv], w=["psg"])
                op_act(S, b1[:, 0:1], psg[:, 0:1], AF.Copy, r=["psg"], w=["b1"])
                op_act(S, zt[:, 0:127], psx[:, 0:127], AF.Identity, r=["psx", "b1"], w=["zt"], bias=b1[:, 0:1])
                op_tt(S, "dve", ut[:, 0:127], zt[:, 0:127], zt[:, 0:127], ALU.mult, r=["zt"], w=["ut"])
                op_ts(S, "dve", ut[:, 0:127], ut[:, 0:127], 0.044715, 1.0, ALU.mult, ALU.add, r=["ut"], w=["ut"])
                op_tt(S, "dve", ut[:, 0:127], ut[:, 0:127], zt[:, 0:127], ALU.mult, r=["ut", "zt"], w=["ut"])
                op_act(S, sgt[:, 0:127], ut[:, 0:127], AF.Sigmoid, r=["ut"], w=["sgt"], scale=1.5957691216057308)
                op_tt(S, "dve", hid[kv][:, fc, 0:127], zt[:, 0:127], sgt[:, 0:127], ALU.mult, r=["zt", "sgt"], w=["hid" + kv])
        for fc in range(2):
            op_mm(S, psx[:, 0:127], w2["k"][:, fc, :], hid["k"][:, fc, 0:127], fc == 0, fc == 1, r=["w2k", "hidk"], w=["psx"])
        op_act(S, kcmp[:, 0:127], psx[:, 0:127], AF.Copy, r=["psx"], w=["kcmp"])
        for fc in range(2):
            op_mm(S, psx[0:127, 0:128], hid["v"][:, fc, 0:127], w2["v"][:, fc, :], fc == 0, fc == 1, r=["w2v", "hidv"], w=["psx"])
        op_act(S, vcmp[0:127, :], psx[0:127, 0:128], AF.Copy, r=["psx"], w=["vcmp"])
        S.dma("sp", Dm["cmpK"][g, :, :], kcmp[:, :], reads=["kcmp"])
        S.dma("sp", Dm["cmpV"][g, :, :], vcmp[:, :], reads=["vcmp"])
    ph.run()


def phase_nsa(nc, cfg, C, Dm, l, groups=(0, 1)):
    S_ = cfg.S
    NTB, NTT = S_ // 512, S_ // 128
    fi = fm_index(cfg)
    scale = HD ** -0.5
    ph = Phase(nc)
    S = ph.S
    A = Attn(ph, C)
    psg = ph.P("psg", [128, 512], F32)
    psx = psg
    ag = ph.T("ag", [24, S_], BF16)
    S.dma("sp", ag[:, :], Dm["agT"][:, :], writes=["ag"])
    kcmp = ph.T("kcmp", [128, 128], BF16)
    vcmp = ph.T("vcmp", [128, 128], BF16)
    ks = ph.T("ks", [128, S_], BF16)
    kw = ph.T("kw", [128, S_], BF16)
    vs = ph.T("vs", [128, NTT, 128], BF16)
    vw = ph.T("vw", [128, NTT, 128], BF16)
    qraw = [ph.T("qraw%d" % i, [128, S_], BF16) for i in range(2)]
    qrot = [ph.T("qrot%d" % i, [128, S_], BF16) for i in range(2)]
    acc = ph.T("acc", [128, 4, S_], F32)
    impsum = ph.T("impsum", [128, 8, 32], F32)
    rcol = ph.T("rcol", [128, 4], F32)
    BT = ph.T("BT", [32, S_ - 1024], BF16)
    coef = [ph.T("coef%d" % i, [128, 512], F32) for i in range(2)]
    tmp = [ph.T("tmp%d" % i, [128, 512], F32) for i in range(2)]
    scw = ph.T("scw", [128, 32], F32)
    wk = ph.T("wk", [128, 32], F32)
    m8 = ph.T("m8", [128, 16], F32)
    selt = ph.T("selt", [128, 32], F32)
    biasb = ph.T("biasb", [128, 32], BF16)
    ostg = [ph.T("ostg%d" % i, [128, S_], BF16) for i in range(2)]
    cnt = dict(c=0)

    gb = [ph.T("gb%d" % i, [128, 3, S_], BF16) for i in range(2)]

    def gate_bcast(h, br):
        for I in range(NTB):
            qs = slice(I * 512, (I + 1) * 512)
            op_mm(S, psg[:, :], C["gsel"][0:24, 3 * h + br, :], ag[0:24, qs], True, True, r=["ag"], w=["psg"])
            op_act(S, gb[h % 2][:, br, qs], psg[:, :], AF.Copy, r=["psg"], w=["gb%d_%d" % (h % 2, br)])

    def combine(pso, pson, rec, recn, h4, I, row, first):
        qs = slice(I * 512, (I + 1) * 512)
        h, br = divmod(row, 3)
        ci = cnt["c"] % 2
        cnt["c"] += 1
        op_tt(S, "dve", coef[ci][:, :], gb[h % 2][:, br, qs], rec[:, :], ALU.mult, r=["gb%d_%d" % (h % 2, br), recn], w=["coef%d" % ci])
        an = "acc%d_%d" % (h4, I)
        if first:
            op_tt(S, "dve", acc[:, h4, qs], pso[:, :], coef[ci][:, :], ALU.mult, r=[pson, "coef%d" % ci], w=[an])
        else:
            op_tt(S, "dve", tmp[ci][:, :], pso[:, :], coef[ci][:, :], ALU.mult, r=[pson, "coef%d" % ci], w=["tmp%d" % ci])
            op_tt(S, "pool", acc[:, h4, qs], acc[:, h4, qs], tmp[ci][:, :], ALU.add, r=[an, "tmp%d" % ci], w=[an])

    for g in groups:
        S.dma("sp", kcmp[:, :], Dm["cmpK"][g, :, :], writes=["kcmp"])
        S.dma("sp", vcmp[:, :], Dm["cmpV"][g, :, :], writes=["vcmp"])
        load_fm(S, "sp", ks[:, :], Dm, fi["a_ks"] + g, ["ks"])
        load_fm(S, "sp", kw[:, :], Dm, fi["a_kw"] + g, ["kw"])
        load_v(S, "sp", vs[:, :, :], Dm["vA"], g * 128, ["vs"])
        load_v(S, "sp", vw[:, :, :], Dm["vA"], 256 + g * 128, ["vw"])
        heads = [4 * g + r for r in range(4)]
        for r4, h in enumerate(heads):
            qb = h % 2
            load_fm(S, "sp", qraw[qb][:, :], Dm, fi["a_q_raw"] + h, ["qraw%d" % qb])
            gate_bcast(h, 0)
            for I in range(NTB):
                qs = slice(I * 512, (I + 1) * 512)

                def extra(eb, ebn, I=I, r4=r4):
                    if I < 2:
                        return
                    for t4 in range(4):
                        tl = 4 * I + t4 - 8
                        op_mm(S, psx[:, t4 * 64:t4 * 64 + 33], eb[0:127, t4 * 128:(t4 + 1) * 128], C["ovx"][0:127, 0:33], True, True,
                              r=[ebn], w=["psg"])
                    for t4 in range(4):
                        tl = 4 * I + t4 - 8
                        op_ts(S, "dve", rcol[:, t4:t4 + 1], psx[:, t4 * 64 + 32:t4 * 64 + 33], 1e-30, None, ALU.max, r=["psg"], w=["rcol"])
                        S.op("dve", lambda e, t4=t4: e.reciprocal(out=rcol[:, t4:t4 + 1], in_=rcol[:, t4:t4 + 1]), reads=["rcol"], writes=["rcol"])
                        if r4 == 0:
                            op_ts(S, "dve", impsum[:, tl, :], psx[:, t4 * 64:t4 * 64 + 32], rcol[:, t4:t4 + 1], None, ALU.mult,
                                  r=["psg", "rcol"], w=["imp%d" % tl])
                        else:
                            S.op("dve", lambda e, t4=t4, tl=tl: e.scalar_tensor_tensor(
                                out=impsum[:, tl, :], in0=psx[:, t4 * 64:t4 * 64 + 32], scalar=rcol[:, t4:t4 + 1],
                                in1=impsum[:, tl, :], op0=ALU.mult, op1=ALU.add), reads=["psg", "rcol", "imp%d" % tl], writes=["imp%d" % tl])

                tiles = [dict(q=qraw[qb][:, qs], k=kcmp[:, 0:127], v=vcmp[0:127, :], nk=127,
                              reads=["qraw%d" % qb, "kcmp", "vcmp"], mask=C["cmask"][0:127, qs], extra=extra)]
                A.block(tiles, scale, lambda pso, pson, rec, recn, r4=r4, I=I, h=h: combine(pso, pson, rec, recn, r4, I, 3 * h + 0, True))

        for tl in range(NTT - 8):
            op_tt(S, "dve", scw[:, :], impsum[:, tl, :], C["cand"][:, tl, :], ALU.mult, r=["imp%d" % tl], w=["scw"])
            op_tt(S, "dve", scw[:, :], scw[:, :], C["negbig"][:, tl, :], ALU.add, r=["scw"], w=["scw"])
            S.op("dve", lambda e: e.max(out=m8[:, 0:8], in_=scw[:, :]), reads=["scw"], writes=["m8a"])
            S.op("dve", lambda e: e.match_replace(out=wk[:, :], in_to_replace=m8[:, 0:8], in_values=scw[:, :], imm_value=-BIG),
                 reads=["scw", "m8a"], writes=["wk"])
            S.op("dve", lambda e: e.max(out=m8[:, 8:16], in_=wk[:, :]), reads=["wk"], writes=["m8b"])
            op_ts(S, "dve", selt[:, :], scw[:, :], m8[:, 12:13], None, ALU.is_ge, r=["scw", "m8b"], w=["selt"])
            op_tt(S, "dve", selt[:, :], selt[:, :], C["forced"][:, tl, :], ALU.max, r=["selt"], w=["selt"])
            op_ts(S, "dve", biasb[:, :], selt[:, :], -1.0, -NEGB, ALU.add, ALU.mult, r=["selt"], w=["biasb"])
            op_mm(S, psx[0:32, 0:128], biasb[:, 0:32], C["ident"][:, :], True, True, r=["biasb"], w=["psg"])
            op_act(S, BT[0:32, tl * 128:(tl + 1) * 128], psx[0:32, 0:128], AF.Copy, r=["psg"], w=["BT"])

        for r4, h in enumerate(heads):
            qb = h % 2
            load_fm(S, "sp", qrot[qb][:, :], Dm, fi["a_q"] + h, ["qrot%d" % qb])
            gate_bcast(h, 1)
            gate_bcast(h, 2)
            for I in range(NTB):
                qs = slice(I * 512, (I + 1) * 512)
                tiles = []
                for j in range(4 * I + 4):
                    t = dict(q=qrot[qb][:, qs], k=ks[:, j * 128:(j + 1) * 128], v=vs[:, j, :], nk=128,
                             reads=["qrot%d" % qb, "ks", "vs"])
                    if j >= 4 * I:
                        t["mask"] = mask_ap(C, 512 * I - 128 * j, None, 1)
                    if I >= 2:
                        t["bias"] = (C["E32"][0:32, j * 128:(j + 1) * 128], BT[0:32, (I - 2) * 512:(I - 1) * 512], ["BT"])
                    tiles.append(t)
                A.block(tiles, scale, lambda pso, pson, rec, recn, r4=r4, I=I, h=h: combine(pso, pson, rec, recn, r4, I, 3 * h + 1, False))
                tiles = []
                for j in range(max(0, 4 * I - 4), 4 * I + 4):
                    off = 512 * I - 128 * j
                    tiles.append(dict(q=qrot[qb][:, qs], k=kw[:, j * 128:(j + 1) * 128], v=vw[:, j, :], nk=128,
                                      reads=["qrot%d" % qb, "kw", "vw"],
                                      mask=mask_ap(C, off, None, 1) if off <= 0 else mask_ap(C, off, 512, 1)))

                def fin(pso, pson, rec, recn, r4=r4, I=I, h=h):
                    combine(pso, pson, rec, recn, r4, I, 3 * h + 2, False)
                    if I == NTB - 1:
                        oi = h % 2
                        accn = ["acc%d_%d" % (r4, I2) for I2 in range(NTB)]
                        S.op("act", lambda e: e.activation(out=ostg[oi][:, :], in_=acc[:, r4, :], func=AF.Copy),
                             reads=accn, writes=["ostg%d" % oi])
                        S.dma("sp", Dm["oT"][h, :, :], ostg[oi][:, :], reads=["ostg%d" % oi], writes=["ostg%d" % oi])

                A.block(tiles, scale, fin)
        A.flush()
    ph.run()


def phase_dil(nc, cfg, C, Dm, l, slots=(0, 1, 2, 3)):
    S_ = cfg.S
    NTB, NTT = S_ // 512, S_ // 128
    fi = fm_index(cfg)
    scale = HD ** -0.5
    ph = Phase(nc)
    S = ph.S
    A = Attn(ph, C)
    q = [[ph.T("q%d_%d" % (p, g), [128, S_], BF16) for g in range(3)] for p in range(2)]
    k = [[ph.T("k%d_%d" % (p, g), [128, S_], BF16) for g in range(3)] for p in range(2)]
    v = [[ph.T("v%d_%d" % (p, g), [128, NTT, 128], BF16) for g in range(3)] for p in range(2)]
    ostg = [ph.T("ostg%d" % i, [128, S_], BF16) for i in range(2)]
    for si, j in enumerate(slots):
        p = si % 2
        for g in range(3):
            hh = 4 * g + j
            load_fm(S, "sp", q[p][g][:, :], Dm, fi["b_q"] + hh, ["q%d_%d" % (p, g)])
            load_fm(S, "sp", k[p][g][:, :], Dm, fi["b_k"] + hh, ["k%d_%d" % (p, g)])
            load_v(S, "sp", v[p][g][:, :, :], Dm["vB"], hh * 128, ["v%d_%d" % (p, g)])
        for I in range(NTB):
            qs = slice(I * 512, (I + 1) * 512)
            tiles = []
            for g, (W, dil, lo) in enumerate(((129, 1, 4 * I - 1), (513, 4, 4 * I - 4), (None, 16, 0))):
                for jj in range(max(0, lo), 4 * I + 4):
                    off = 512 * I - 128 * jj
                    if g == 2 and off > 0:
                        off = 4096
                    tiles.append(dict(q=q[p][g][:, qs], k=k[p][g][:, jj * 128:(jj + 1) * 128], v=v[p][g][:, jj, :], nk=128,
                                      reads=["q%d_%d" % (p, g), "k%d_%d" % (p, g), "v%d_%d" % (p, g)],
                                      mask=mask_ap(C, off, W, dil)))

            def fin(pso, pson, rec, recn, p=p, I=I, j=j, qs=qs):
                op_tt(S, "dve", ostg[p][:, qs], pso[:, :], rec[:, :], ALU.mult, r=[pson, recn], w=["ostg%d_%d" % (p, I)])
                if I == NTB - 1:
                    names = ["ostg%d_%d" % (p, I2) for I2 in range(NTB)]
                    S.dma("sp", Dm["oT"][8 + j, :, :], ostg[p][:, :], reads=names, writes=names)

            A.block(tiles, scale, fin)
    A.flush()
    ph.run()


def phase_moba(nc, cfg, C, Dm, l, heads=tuple(range(8))):
    S_ = cfg.S
    NTB, NTT = S_ // 512, S_ // 128
    NBK = S_ // 256
    fi = fm_index(cfg)
    scale = HD ** -0.5
    ph = Phase(nc)
    S = ph.S
    A = Attn(ph, C)
    psx = ph.P("psx", [128, 512], F32)
    q = [ph.T("q%d" % p, [128, S_], BF16) for p in range(2)]
    k = [ph.T("k%d" % p, [128, S_], BF16) for p in range(2)]
    v = [ph.T("v%d" % p, [128, NTT, 128], BF16) for p in range(2)]
    ostg = [ph.T("ostg%d" % i, [128, S_], BF16) for i in range(2)]
    km = ph.T("km", [128, 8], F32)
    kmb = ph.T("kmb", [128, 8], BF16)
    gm = ph.T("gm", [128, NTT, 8], F32)
    m8 = ph.T("m8", [128, NTT, 8], F32)
    sel = ph.T("sel", [128, NTT, 8], F32)
    okt = ph.T("okt", [128, NTT, 8], F32)
    biasb = ph.T("biasb", [128, NTT, 8], BF16)
    BT = ph.T("BT", [8, S_], BF16)
    for hi, h in enumerate(heads):
        p = hi % 2
        qn, kn, vn = "q%d" % p, "k%d" % p, "v%d" % p
        load_fm(S, "sp", q[p][:, :], Dm, fi["c_q"] + h, [qn])
        load_fm(S, "sp", k[p][:, :], Dm, fi["c_k"] + h, [kn])
        load_v(S, "sp", v[p][:, :, :], Dm["vC"], h * 128, [vn])
        S.op("dve", lambda e, p=p: e.reduce_sum(out=km[:, 0:NBK], in_=k[p][:, :].rearrange("p (b s) -> p b s", s=256), axis=AX.X),
             reads=[kn], writes=["km"])
        op_ts(S, "dve", kmb[:, 0:NBK], km[:, 0:NBK], 1.0 / 256.0, None, ALU.mult, r=["km"], w=["kmb"])
        for tt in range(NTT):
            op_mm(S, psx[:, tt * 8:tt * 8 + NBK], q[p][:, tt * 128:(tt + 1) * 128], kmb[:, 0:NBK], True, True, r=[qn, "kmb"], w=["psx"])
        op_tt(S, "dve", gm[:, :, 0:NBK], psx[:, 0:NTT * 8].rearrange("p (t j) -> p t j", j=8)[:, :, 0:NBK], C["pastneg"][:, 0:NTT, 0:NBK], ALU.add,
              r=["psx"], w=["gm"])
        for tt in range(NTT):
            S.op("dve", lambda e, tt=tt: e.max(out=m8[:, tt, :], in_=gm[:, tt, :]), reads=["gm"], writes=["m8"])
        op_tt(S, "dve", sel[:, :, :], gm[:, :, :], m8[:, :, 2:3].to_broadcast([128, NTT, 8]), ALU.is_ge, r=["gm", "m8"], w=["sel"])
        op_ts(S, "dve", okt[:, :, :], gm[:, :, :], -0.5 * BIG, None, ALU.is_gt, r=["gm"], w=["okt"])
        op_tt(S, "dve", sel[:, :, :], sel[:, :, :], okt[:, :, :], ALU.mult, r=["sel", "okt"], w=["sel"])
        op_tt(S, "dve", sel[:, :, :], sel[:, :, :], C["own"][:, 0:NTT, :], ALU.max, r=["sel"], w=["sel"])
        op_ts(S, "dve", biasb[:, :, :], sel[:, :, :], -1.0, -NEGB, ALU.add, ALU.mult, r=["sel"], w=["biasb"])
        for I in range(NTB):
            for t4 in range(4):
                tt = 4 * I + t4
                op_mm(S, psx[0:8, t4 * 128:(t4 + 1) * 128], biasb[:, tt, :], C["ident"][:, :], True, True, r=["biasb"], w=["psx"])
            op_act(S, BT[0:8, I * 512:(I + 1) * 512], psx[0:8, :], AF.Copy, r=["psx"], w=["BT%d" % I])
        for I in range(NTB):
            qs = slice(I * 512, (I + 1) * 512)
            tiles = []
            for j in range(4 * I + 4):
                t = dict(q=q[p][:, qs], k=k[p][:, j * 128:(j + 1) * 128], v=v[p][:, j, :], nk=128, reads=[qn, kn, vn],
                         bias=(C["E8"][0:8, j * 128:(j + 1) * 128], BT[0:8, qs], ["BT%d" % I]))
                if j >= 4 * I:
                    t["mask"] = mask_ap(C, 512 * I - 128 * j, None, 1)
                tiles.append(t)

            def fin(pso, pson, rec, recn, p=p, I=I, h=h, qs=qs):
                op_tt(S, "dve", ostg[p][:, qs], pso[:, :], rec[:, :], ALU.mult, r=[pson, recn], w=["ostg%d_%d" % (p, I)])
                if I == NTB - 1:
                    names = ["ostg%d_%d" % (p, I2) for I2 in range(NTB)]
                    S.dma("sp", Dm["oT"][12 + h, :, :], ostg[p][:, :], reads=names, writes=names)

            A.block(tiles, scale, fin)
    A.flush()
    ph.run()


def phase_merge(nc, cfg, C, Dm, l, SL):
    D, KC = cfg.D, cfg.KC
    ntb = SL // 512
    fi = fm_index(cfg)
    ph = Phase(nc)
    S = ph.S
    wbr = ph.T("wbr", [128, 20, D], BF16)
    S.dma("pool", wbr[:, 0:8, :], Dm["w_br_a"][l].rearrange("(h p) d -> p h d", p=128), writes=["wbr"])
    S.dma("pool", wbr[:, 8:12, :], Dm["w_br_b"][l].rearrange("(h p) d -> p h d", p=128), writes=["wbr"])
    S.dma("pool", wbr[:, 12:20, :], Dm["w_br_c"][l].rearrange("(h p) d -> p h d", p=128), writes=["wbr"])
    oblk = [ph.T("oblk%d" % i, [128, 20, 512], BF16) for i in range(2)]
    mg = [ph.T("mg%d" % i, [128, 3, 512], BF16) for i in range(2)]
    mstg = [ph.T("mstg%d" % i, [128, KC, 512], BF16) for i in range(2)]
    ta = [ph.T("ta%d" % i, [128, 512], F32) for i in range(2)]
    tb_ = [ph.T("tb%d" % i, [128, 512], F32) for i in range(2)]
    ps = [[ph.P("ps%d_%d" % (i, j), [128, 512], F32) for j in range(3)] for i in range(2)]
    n = 0
    for tb in range(ntb):
        ob = tb % 2
        ts_ = slice(tb * 512, (tb + 1) * 512)
        S.dma("sp", oblk[ob][:, :, :], Dm["oT"][:, :, ts_].rearrange("h p t -> p h t"), writes=["oblk%d" % ob])
        msn = "mstg%d" % ob
        for dc in range(KC):
            b = n % 2
            n += 1
            for j, nm in enumerate(("m_a", "m_b", "m_c")):
                S.dma("sp", mg[b][:, j, :], Dm["projT"][fi[nm] + dc, :, ts_], writes=["mg%d_%d" % (b, j)])
            for j, (h0, h1) in enumerate(((0, 8), (8, 12), (12, 20))):
                for h in range(h0, h1):
                    op_mm(S, ps[b][j][:, :], wbr[:, h, dc * 128:(dc + 1) * 128], oblk[ob][:, h, :], h == h0, h == h1 - 1,
                          r=["wbr", "oblk%d" % ob], w=["ps%d_%d" % (b, j)])
            op_tt(S, "dve", ta[b][:, :], ps[b][0][:, :], mg[b][:, 0, :], ALU.mult, r=["ps%d_0" % b, "mg%d_0" % b], w=["ta%d" % b])
            op_tt(S, "dve", tb_[b][:, :], ps[b][1][:, :], mg[b][:, 1, :], ALU.mult, r=["ps%d_1" % b, "mg%d_1" % b], w=["tb%d" % b])
            op_tt(S, "dve", ta[b][:, :], ta[b][:, :], tb_[b][:, :], ALU.add, r=["ta%d" % b, "tb%d" % b], w=["ta%d" % b])
            op_tt(S, "dve", tb_[b][:, :], ps[b][2][:, :], mg[b][:, 2, :], ALU.mult, r=["ps%d_2" % b, "mg%d_2" % b], w=["tb%d" % b])
            op_tt(S, "dve", mstg[ob][:, dc, :], ta[b][:, :], tb_[b][:, :], ALU.add, r=["ta%d" % b, "tb%d" % b], w=[msn])
        S.dma("sp", Dm["mergedT"][:, :, ts_].rearrange("k p t -> p k t"), mstg[ob][:, :, :], reads=[msn], writes=[msn])
    ph.run()


def phase_wo(nc, cfg, C, Dm, l, x_src, x_dst, SL):
    D, KC = cfg.D, cfg.KC
    ntt = SL // 128
    CW = min(512, D)
    ph = Phase(nc)
    S = ph.S
    wo = ph.T("wo", [128, KC, D], BF16)
    S.dma("pool", wo[:, :, :], Dm["w_o"][l].rearrange("(k p) n -> p k n", p=128), writes=["wo"])
    mblk = [ph.T("mblk%d" % i, [128, KC, 512], BF16) for i in range(2)]
    xt = [ph.T("xt%d" % i, [128, D], F32) for i in range(3)]
    ps = [ph.P("ps%d" % i, [128, 512], F32) for i in range(4)]
    n = 0

    def load(tt):
        tb, t4 = divmod(tt, 4)
        if t4 == 0:
            S.dma("sp", mblk[tb % 2][:, :, :], Dm["mergedT"][:, :, tb * 512:(tb + 1) * 512].rearrange("k p t -> p k t"),
                  writes=["mblk%d" % (tb % 2)])
        S.dma("sp", xt[tt % 3][:, :], x_src[tt * 128:(tt + 1) * 128, :], writes=["xt%d" % (tt % 3)])

    load(0)
    for tt in range(ntt):
        if tt + 1 < ntt:
            load(tt + 1)
        tb, t4 = divmod(tt, 4)
        mb, xb = tb % 2, tt % 3
        for cb in range(D // CW):
            pi_ = n % 4
            n += 1
            for kc in range(KC):
                op_mm(S, ps[pi_][:, 0:CW], mblk[mb][:, kc, t4 * 128:(t4 + 1) * 128], wo[:, kc, cb * CW:(cb + 1) * CW],
                      kc == 0, kc == KC - 1, r=["wo", "mblk%d" % mb], w=["ps%d" % pi_])
            op_tt(S, "dve", xt[xb][:, cb * CW:(cb + 1) * CW], xt[xb][:, cb * CW:(cb + 1) * CW], ps[pi_][:, 0:CW], ALU.add,
                  r=["xt%d" % xb, "ps%d" % pi_], w=["xt%d" % xb])
        S.dma("sp", x_dst[tt * 128:(tt + 1) * 128, :], xt[xb][:, :], reads=["xt%d" % xb], writes=["xt%d" % xb])
    ph.run()


def phase_mlp_in(nc, cfg, C, Dm, l, hT, SL):
    KC, FC = cfg.KC, cfg.FC
    ntb, ntt = SL // 512, SL // 128
    ph = Phase(nc)
    S = ph.S
    NW = 3
    w1b = [ph.T("w1b%d" % i, [128, KC, 512], BF16) for i in range(NW)]
    ustg = [ph.T("ustg%d" % i, [128, SL], BF16) for i in range(3)]
    rt = [ph.T("rt%d" % i, [128, 512], F32) for i in range(3)]
    psi = [ph.P("psi%d" % i, [128, 512], F32) for i in range(4)]
    n2 = n3 = 0
    for fs in range(FC // 4):
        wi = fs % NW
        S.dma("pool", w1b[wi][:, :, :], Dm["w_mlp_in"][l, :, fs * 512:(fs + 1) * 512].rearrange("(k p) n -> p k n", p=128),
              writes=["w1b%d" % wi])
        for ci in range(4):
            fc = 4 * fs + ci
            ui = fc % 3
            un = "ustg%d" % ui
            for tb in range(ntb):
                ts_ = slice(tb * 512, (tb + 1) * 512)
                pi_ = n2 % 4
                n2 += 1
                for kc in range(KC):
                    op_mm(S, psi[pi_][:, :], w1b[wi][:, kc, ci * 128:(ci + 1) * 128], hT[:, kc, ts_], kc == 0, kc == KC - 1,
                          r=["w1b%d" % wi], w=["psi%d" % pi_])
                ri = n3 % 3
                n3 += 1
                op_act(S, rt[ri][:, :], psi[pi_][:, :], AF.Relu, r=["psi%d" % pi_], w=["rt%d" % ri])
                op_tt(S, "dve", ustg[ui][:, ts_], rt[ri][:, :], rt[ri][:, :], ALU.mult, r=["rt%d" % ri], w=[un + "_%d" % tb])
            names = [un + "_%d" % tb for tb in range(ntb)]
            S.dma("sp", Dm["uS"][:, :, fc, :].rearrange("t p c -> p t c"), ustg[ui][:, :].rearrange("p (t c) -> p t c", c=128),
                  reads=names, writes=names)
    ph.run()


def phase_mlp_out(nc, cfg, C, Dm, l, xres, SL):
    D, FC = cfg.D, cfg.FC
    ntt = SL // 128
    CW = min(512, D)
    FQ = max(1, FC // 4)
    ph = Phase(nc)
    S = ph.S
    w2c = [ph.T("w2c%d" % i, [128, FC, CW], BF16) for i in range(2)]
    ub = [ph.T("ub%d" % i, [128, FC, 128], BF16) for i in range(3)]
    xs = [ph.T("xs%d" % i, [128, CW], F32) for i in range(3)]
    pso = [ph.P("pso%d" % i, [128, 512], F32) for i in range(4)]
    items = [(cb, tt) for cb in range(D // CW) for tt in range(ntt)]

    def load(i):
        cb, tt = items[i]
        wi = cb % 2
        if tt == 0:
            for q4 in range(FC // FQ):
                S.dma("pool", w2c[wi][:, q4 * FQ:(q4 + 1) * FQ, :],
                      Dm["w_mlp_out"][l, q4 * FQ * 128:(q4 + 1) * FQ * 128, cb * CW:(cb + 1) * CW].rearrange("(f p) n -> p f n", p=128),
                      writes=["w2c%d_%d" % (wi, q4)])
        S.dma("sp", ub[i % 3][:, :, :], Dm["uS"][tt, :, :, :], writes=["ub%d" % (i % 3)])
        S.dma("sp", xs[i % 3][:, :], xres[tt * 128:(tt + 1) * 128, cb * CW:(cb + 1) * CW], writes=["xs%d" % (i % 3)])

    load(0)
    for i, (cb, tt) in enumerate(items):
        if i + 1 < len(items):
            load(i + 1)
        wi, ui, pi_ = cb % 2, i % 3, i % 4
        for fc in range(FC):
            op_mm(S, pso[pi_][:, 0:CW], ub[ui][:, fc, :], w2c[wi][:, fc, :], fc == 0, fc == FC - 1,
                  r=["ub%d" % ui, "w2c%d_%d" % (wi, fc // FQ)], w=["pso%d" % pi_])
        op_tt(S, "dve", xs[ui][:, :], xs[ui][:, :], pso[pi_][:, 0:CW], ALU.add, r=["xs%d" % ui, "pso%d" % pi_], w=["xs%d" % ui])
        S.dma("sp", xres[tt * 128:(tt + 1) * 128, cb * CW:(cb + 1) * CW], xs[ui][:, :], reads=["xs%d" % ui], writes=["xs%d" % ui])
    ph.run()


def phase_final(nc, cfg, C, x_src, g_ap, out_ap, SL):
    D = cfg.D
    ntt = SL // 128
    ph = Phase(nc)
    S = ph.S
    gB = ph.T("gB", [128, D], F32)
    S.dma("sp", gB[:, :], g_ap.partition_broadcast(128), writes=["gB"])
    xt = [ph.T("xt%d" % i, [128, D], F32) for i in range(3)]
    junk = ph.T("junk", [128, D], BF16)
    st = ph.T("st", [128, ntt, 4], F32)
    S.op("dve", lambda e: e.memset(st[:], 0.0), writes=["st"])
    S.dma("sp", xt[0][:, :], x_src[0:128, :], writes=["xt0"])
    for tt in range(ntt):
        b = tt % 3
        xb = "xt%d" % b
        if tt + 1 < ntt:
            S.dma("sp", xt[(tt + 1) % 3][:, :], x_src[(tt + 1) * 128:(tt + 2) * 128, :], writes=["xt%d" % ((tt + 1) % 3)])
        op_act(S, junk[:, :], xt[b][:, :], AF.Square, r=[xb, "st"], w=["junk", "s0_%d" % tt], accum_out=st[:, tt, 0:1])
        op_ts(S, "dve", st[:, tt, 1:2], st[:, tt, 0:1], 1.0 / D, 1e-6, ALU.mult, ALU.add, r=["s0_%d" % tt], w=["s1_%d" % tt])
        op_act(S, st[:, tt, 2:3], st[:, tt, 1:2], AF.Sqrt, r=["s1_%d" % tt], w=["s2_%d" % tt])
        S.op("dve", lambda e, tt=tt: e.reciprocal(out=st[:, tt, 3:4], in_=st[:, tt, 2:3]), reads=["s2_%d" % tt], writes=["s3_%d" % tt])
        op_ts(S, "dve", xt[b][:, :], xt[b][:, :], st[:, tt, 3:4], None, ALU.mult, r=[xb, "s3_%d" % tt], w=[xb])
        op_tt(S, "pool", xt[b][:, :], xt[b][:, :], gB[:, :], ALU.mult, r=[xb, "gB"], w=[xb])
        S.dma("sp", out_ap[tt * 128:(tt + 1) * 128, :], xt[b][:, :], reads=[xb], writes=[xb])
    ph.run()


INPUT_SHAPES = lambda c: {
    "x": [c.S, c.D],
    "attn_norm_g": [c.DEPTH, c.D],
    "w_in": [c.DEPTH, c.D, c.INW],
    "cmp_pe_k": [c.DEPTH, 32, 128], "cmp_w1_k": [c.DEPTH, 32, 128, 256], "cmp_w2_k": [c.DEPTH, 256, 128],
    "cmp_pe_v": [c.DEPTH, 32, 128], "cmp_w1_v": [c.DEPTH, 32, 128, 256], "cmp_w2_v": [c.DEPTH, 256, 128],
    "w_br_a": [c.DEPTH, 1024, c.D], "w_br_b": [c.DEPTH, 512, c.D], "w_br_c": [c.DEPTH, 1024, c.D],
    "w_o": [c.DEPTH, c.D, c.D],
    "mlp_norm_g": [c.DEPTH, c.D],
    "w_mlp_in": [c.DEPTH, c.D, c.DFF], "w_mlp_out": [c.DEPTH, c.DFF, c.D],
    "final_norm_g": [c.D],
}


def build(cfg, stop_after=None):
    nc = bass.Bass("TRN2", target_bir_lowering=False)
    Dm = {}
    for name, shp in INPUT_SHAPES(cfg).items():
        Dm[name] = nc.dram_tensor(name, shp, F32, kind="ExternalInput").ap()
    out = nc.dram_tensor("out", [cfg.S, cfg.D], F32, kind="ExternalOutput").ap()
    fi = fm_index(cfg)
    S_, D = cfg.S, cfg.D
    Dm["xres"] = nc.dram_tensor("xres", [S_, D], F32).ap()
    Dm["projT"] = nc.dram_tensor("projT", [fi["_n"], 128, S_], BF16).ap()
    Dm["agT"] = nc.dram_tensor("agT", [24, S_], BF16).ap()
    Dm["vA"] = nc.dram_tensor("vA", [S_, 512], BF16).ap()
    Dm["vB"] = nc.dram_tensor("vB", [S_, 1536], BF16).ap()
    Dm["vC"] = nc.dram_tensor("vC", [S_, 1024], BF16).ap()
    Dm["cmpK"] = nc.dram_tensor("cmpK", [2, 128, 128], BF16).ap()
    Dm["cmpV"] = nc.dram_tensor("cmpV", [2, 128, 128], BF16).ap()
    Dm["oT"] = nc.dram_tensor("oT", [20, 128, S_], BF16).ap()
    Dm["mergedT"] = nc.dram_tensor("mergedT", [cfg.KC, 128, S_], BF16).ap()
    Dm["uS"] = nc.dram_tensor("uS", [S_ // 128, 128, cfg.FC, 128], BF16).ap()
    C = {}
    with contextlib.ExitStack() as gst:
        build_consts(nc, cfg, C, gst, "global")
        for l in range(cfg.DEPTH):
            x_src = Dm["x"] if l == 0 else Dm["xres"]
            with contextlib.ExitStack() as st:
                hT = st.enter_context(nc.sbuf_tensor("hT_a%d" % l, [128, cfg.KC, S_], BF16))
                build_consts(nc, cfg, C, st, "rope")
                phase_norm(nc, cfg, C, x_src, Dm["attn_norm_g"][l], hT, S_ // 128)
                phase_inproj(nc, cfg, C, Dm, l, hT, S_, 0)
            if stop_after == "inproj":
                break
            with contextlib.ExitStack() as st:
                build_consts(nc, cfg, C, st, "attn")
                phase_nsa_compress(nc, cfg, C, Dm, l)
                phase_nsa(nc, cfg, C, Dm, l)
                phase_dil(nc, cfg, C, Dm, l)
                phase_moba(nc, cfg, C, Dm, l)
            if stop_after == "attn":
                break
            phase_merge(nc, cfg, C, Dm, l, S_)
            phase_wo(nc, cfg, C, Dm, l, x_src, Dm["xres"], S_)
            with contextlib.ExitStack() as st:
                hT = st.enter_context(nc.sbuf_tensor("hT_m%d" % l, [128, cfg.KC, S_], BF16))
                phase_norm(nc, cfg, C, Dm["xres"], Dm["mlp_norm_g"][l], hT, S_ // 128)
                phase_mlp_in(nc, cfg, C, Dm, l, hT, S_)
            phase_mlp_out(nc, cfg, C, Dm, l, Dm["xres"], S_)
        if stop_after is None:
            phase_final(nc, cfg, C, Dm["xres"], Dm["final_norm_g"], out, S_)
    return nc


_NC_CACHE = {}


def kernel(**inputs):
    cfg = Cfg()
    x = np.asarray(inputs["x"], dtype=np.float32)
    B = x.shape[0]
    if "nc" not in _NC_CACHE:
        _NC_CACHE["nc"] = build(cfg)
    nc = _NC_CACHE["nc"]
    shared = {k: np.ascontiguousarray(np.asarray(inputs[k], dtype=np.float32)) for k in INPUT_SHAPES(cfg) if k != "x"}
    n_cores = 8
    in_maps = []
    zeros = np.zeros_like(x[0])
    for c in range(n_cores):
        m = dict(shared)
        m["x"] = np.ascontiguousarray(x[c // 2]) if (c % 2 == 0 and c // 2 < B) else zeros
        in_maps.append(m)
    res = run_bass_kernel_spmd(nc, in_maps, core_ids=list(range(n_cores)))
    return np.stack([np.asarray(res.results[2 * b]["out"], dtype=np.float32) for b in range(B)], axis=0)
```

```python
import contextlib
import math

import numpy as np

import concourse.bass as bass
import concourse.mybir as mybir
from concourse.bass_utils import run_bass_kernel_spmd

F32 = mybir.dt.float32
BF16 = mybir.dt.bfloat16
AF = mybir.ActivationFunctionType
ALU = mybir.AluOpType
AX = mybir.AxisListType

HD = 128
NEGB = -30000.0
BIG = 1.0e30


class Cfg:
    def __init__(self, D=2048, DFF=8192, S=2048, DEPTH=2):
        self.D, self.DFF, self.S, self.DEPTH = D, DFF, S, DEPTH
        self.KC = D // 128
        self.FC = DFF // 128
        self.NTB = S // 512
        self.NTT = S // 128
        o = 0
        self.col = {}
        for name, w in (("a_q", 1024), ("a_kc", 256), ("a_vc", 256), ("a_ks", 256), ("a_vs", 256),
                        ("a_kw", 256), ("a_vw", 256), ("a_g", 24), ("b_q", 1536), ("b_k", 1536),
                        ("b_v", 1536), ("c_q", 1024), ("c_k", 1024), ("c_v", 1024),
                        ("m_a", D), ("m_b", D), ("m_c", D)):
            self.col[name] = (o, w)
            o += w
        self.INW = o


_uid = [0]


class Sched:
    ENGS = ("pe", "act", "dve", "pool", "sp")
    NDMASEM = 8

    def __init__(self, nc, same_engine_sync=("act", "dve", "pool")):
        self.nc = nc
        self.ops = []
        self.last_writer = {}
        self.readers = {}
        self.same_engine_sync = set(same_engine_sync)

    def _add(self, eng, fn, reads, writes, is_dma):
        idx = len(self.ops)
        deps = set()
        for r in reads:
            w = self.last_writer.get(r)
            if w is not None:
                deps.add(w)
        for w_ in writes:
            w = self.last_writer.get(w_)
            if w is not None:
                deps.add(w)
            for r in self.readers.get(w_, ()):
                deps.add(r)
        deps.discard(idx)
        self.ops.append(dict(eng=eng, fn=fn, deps=deps, dma=is_dma))
        for r in reads:
            self.readers.setdefault(r, []).append(idx)
        for w_ in writes:
            self.last_writer[w_] = idx
            self.readers[w_] = []
        return idx

    def op(self, eng, fn, reads=(), writes=()):
        return self._add(eng, fn, tuple(reads), tuple(writes), False)

    def dma(self, eng, out, in_, reads=(), writes=(), slow=False):
        if slow:
            return self._add(eng, lambda e: e.dma_start(out=out, in_=in_, allow_slow_non_contiguous=True),
                             tuple(reads), tuple(writes), True)
        return self._add(eng, lambda e: e.dma_start(out=out, in_=in_), tuple(reads), tuple(writes), True)

    def run(self, stack):
        nc = self.nc
        ops = self.ops
        if not ops:
            return
        needed = [False] * len(ops)
        for i, o in enumerate(ops):
            if o["dma"]:
                needed[i] = True
            keep = set()
            for d in o["deps"]:
                od = ops[d]
                if od["dma"] or o["dma"] or od["eng"] != o["eng"]:
                    keep.add(d)
                elif o["eng"] in self.same_engine_sync:
                    keep.add(d)
            o["deps"] = keep
            for d in keep:
                needed[d] = True
        with nc.cleanup_on_exit():
            self._emit(nc, ops, needed)

    def _emit(self, nc, ops, needed):
        _uid[0] += 1
        u = _uid[0]
        sems = {e: nc.alloc_semaphore(name="s%d_%s" % (u, e)) for e in self.ENGS}
        dsems = {e: [nc.alloc_semaphore(name="d%d_%s%d" % (u, e, k)) for k in range(self.NDMASEM)]
                 for e in ("act", "pool", "sp")}
        cnt = {e: 0 for e in self.ENGS}
        dcnt = {e: 0 for e in dsems}
        duse = {}
        final = {}
        for i, o in enumerate(ops):
            if o["dma"]:
                e = o["eng"]
                s = dsems[e][dcnt[e] % self.NDMASEM]
                dcnt[e] += 1
                duse[s.name] = duse.get(s.name, 0) + 1
                o["sig"] = (s, 16 * duse[s.name])
                o["prev"] = (s, 16 * (duse[s.name] - 1)) if duse[s.name] > 1 else None
                final[s.name] = o["sig"]
            elif needed[i]:
                cnt[o["eng"]] += 1
                o["sig"] = (sems[o["eng"]], cnt[o["eng"]])
        per = {e: [i for i, o in enumerate(ops) if o["eng"] == e] for e in self.ENGS}

        def emit(e, engobj):
            waited = {}

            def wait(s, v):
                if waited.get(s.name, 0) < v:
                    engobj.wait_ge(s, v)
                    waited[s.name] = v

            for i in per[e]:
                o = ops[i]
                for d in sorted(o["deps"]):
                    wait(*ops[d]["sig"])
                if o["dma"] and o["prev"] is not None:
                    wait(*o["prev"])
                ins = o["fn"](engobj)
                if "sig" in o:
                    ins.then_inc(o["sig"][0], 16 if o["dma"] else 1)
            if e == "sp":
                for s, v in final.values():
                    wait(s, v)

        with nc.Block() as block:
            block.sync(lambda en: emit("sp", en))
            if per["pe"]:
                block.tensor(lambda en: emit("pe", en))
            if per["act"]:
                block.scalar(lambda en: emit("act", en))
            if per["dve"]:
                block.vector(lambda en: emit("dve", en))
            if per["pool"]:
                block.gpsimd(lambda en: emit("pool", en))


class Phase:
    def __init__(self, nc):
        self.nc = nc
        self.st = contextlib.ExitStack()
        self.S = Sched(nc)

    def T(self, name, shape, dt):
        _uid[0] += 1
        return self.st.enter_context(self.nc.sbuf_tensor("%s_%d" % (name, _uid[0]), list(shape), dt))

    def P(self, name, shape, dt):
        _uid[0] += 1
        return self.st.enter_context(self.nc.psum_tensor("%s_%d" % (name, _uid[0]), list(shape), dt))

    def run(self):
        self.S.run(self.st)
        self.st.close()


MASK_KEYS = []


def _mask_keys():
    keys = []
    for off in (0, -128, -256, -384):
        keys.append((off, None, 1))
    for off in (128, 256, 384, 512):
        keys.append((off, 512, 1))
    for off in (-384, -256, -128, 0, 128):
        keys.append((off, 129, 1))
    for off in (-384, -256, -128, 0, 128, 256, 384, 512):
        keys.append((off, 513, 4))
    for off in (-384, -256, -128, 0, 4096):
        keys.append((off, None, 16))
    return keys


def build_consts(nc, cfg, C, st, which):
    S_ = cfg.S

    def T(name, shape, dt):
        _uid[0] += 1
        return st.enter_context(nc.sbuf_tensor("%s_%d" % (name, _uid[0]), list(shape), dt))

    ph = Phase(nc)
    S = ph.S
    pi = math.pi

    def ts(eng, out, in0, s1, s2, op0, op1=None, r=(), w=()):
        op_ts(S, eng, out, in0, s1, s2, op0, op1, r=r, w=w)

    def tt(eng, out, in0, in1, op, r=(), w=()):
        op_tt(S, eng, out, in0, in1, op, r=r, w=w)

    def iota(out, pattern, base, cm, w):
        S.op("pool", lambda e: e.iota(out, pattern=pattern, base=base, channel_multiplier=cm,
                                      allow_small_or_imprecise_dtypes=True), writes=w)

    if which == "global":
        C["ident"] = T("c_ident", [128, 128], BF16)
        C["ones"] = T("c_ones", [128, 128], BF16)
        C["pm"] = T("c_pm", [32, 32], BF16)
        C["eps"] = T("c_eps", [128, 1], F32)
    if which == "rope":
        C["ropeC"] = T("c_ropeC", [32, S_], F32)
        C["ropeS"] = T("c_ropeS", [32, S_], F32)
    if which == "attn":
        keys = _mask_keys()
        C["mask_keys"] = {k: i for i, k in enumerate(keys)}
        C["masks"] = T("c_masks", [128, len(keys), 512], BF16)
        C["cmask"] = T("c_cmask", [128, S_], BF16)
        C["E32"] = T("c_E32", [128, S_], BF16)
        C["E8"] = T("c_E8", [128, S_], BF16)
        C["gsel"] = T("c_gsel", [24, 24, 128], BF16)
        C["ovx"] = T("c_ovx", [128, 33], BF16)
        C["forced"] = T("c_forced", [128, 8, 32], F32)
        C["cand"] = T("c_cand", [128, 8, 32], F32)
        C["negbig"] = T("c_negbig", [128, 8, 32], F32)
        C["pastneg"] = T("c_pastneg", [128, 16, 8], F32)
        C["own"] = T("c_own", [128, 16, 8], F32)

    w1 = ph.T("w1", [128, 512], F32)
    w2 = ph.T("w2", [128, 512], F32)
    w3 = ph.T("w3", [128, 512], F32)
    col = ph.T("col", [128, 8], F32)
    coli = ph.T("coli", [128, 2], mybir.dt.int32)
    iota(coli[:, 0:1], [[0, 1]], 0, 1, ["coli0"])
    S.op("dve", lambda e: e.tensor_copy(out=col[:, 0:1], in_=coli[:, 0:1]), reads=["coli0"], writes=["col0"])

    if which == "global":
        iota(w1[:, 0:128], [[1, 128]], 0, -1, ["w1"])
        ts("dve", C["ident"][:], w1[:, 0:128], 0.0, None, ALU.is_equal, r=["w1"], w=["ident"])
        ts("dve", w2[0:32, 0:32], w1[0:32, 0:32], 16.0, None, ALU.is_equal, r=["w1"], w=["w2"])
        ts("dve", w3[0:32, 0:32], w1[0:32, 0:32], -16.0, None, ALU.is_equal, r=["w1"], w=["w3"])
        tt("dve", C["pm"][:], w2[0:32, 0:32], w3[0:32, 0:32], ALU.max, r=["w2", "w3"], w=["pm"])
        S.op("dve", lambda e: e.memset(C["ones"][:], 1.0), writes=["ones"])
        S.op("dve", lambda e: e.memset(C["eps"][:], 1e-30), writes=["eps"])

    if which == "rope":
        big = ph.T("big", [32, S_], F32)
        big2 = ph.T("big2", [32, S_], F32)
        kf = ph.T("kf", [32, S_], F32)
        ki = ph.T("ki", [32, S_], mybir.dt.int32)
        op_ts(S, "dve", coli[:, 1:2], coli[:, 0:1], 15, None, ALU.bitwise_and, r=["coli0"], w=["coli1"])
        S.op("dve", lambda e: e.tensor_copy(out=col[:, 1:2], in_=coli[:, 1:2]), reads=["coli1"], writes=["col1"])
        ts("dve", col[:, 1:2], col[:, 1:2], -1.0 / 16.0, None, ALU.mult, r=["col1"], w=["col1"])
        S.op("dve", lambda e: e.memset(col[:, 5:6], 500000.0), writes=["col5"])
        tt("pool", col[:, 2:3], col[:, 5:6], col[:, 1:2], ALU.pow, r=["col1", "col5"], w=["col2"])
        ts("dve", col[:, 3:4], col[:, 0:1], 16.0, 2.0, ALU.is_ge, ALU.mult, r=["col0"], w=["col3"])
        ts("dve", col[:, 3:4], col[:, 3:4], -1.0, None, ALU.add, r=["col3"], w=["col3"])
        iota(big[:], [[1, S_]], 0, 0, ["big"])
        ts("dve", big2[:], big[:], col[0:32, 2:3], None, ALU.mult, r=["big", "col2"], w=["big2"])

        def sin_of(dst, shift, sgn_col):
            ts("dve", big[:], big2[:], shift, None, ALU.add, r=["big2"], w=["big"])
            ts("dve", kf[:], big[:], 1.0 / (2 * pi), None, ALU.mult, r=["big"], w=["kf"])
            S.op("dve", lambda e: e.tensor_copy(out=ki[:], in_=kf[:]), reads=["kf"], writes=["ki"])
            S.op("dve", lambda e: e.tensor_copy(out=kf[:], in_=ki[:]), reads=["ki"], writes=["kf"])
            S.op("dve", lambda e: e.scalar_tensor_tensor(out=big[:], in0=kf[:], scalar=-2 * pi, in1=big[:], op0=ALU.mult, op1=ALU.add),
                 reads=["kf", "big"], writes=["big"])
            ts("dve", kf[:], big[:], pi, None, ALU.is_gt, r=["big"], w=["kf"])
            S.op("dve", lambda e: e.scalar_tensor_tensor(out=big[:], in0=kf[:], scalar=-2 * pi, in1=big[:], op0=ALU.mult, op1=ALU.add),
                 reads=["kf", "big"], writes=["big"])
            ts("dve", kf[:], big[:], -pi, None, ALU.is_lt, r=["big"], w=["kf"])
            S.op("dve", lambda e: e.scalar_tensor_tensor(out=big[:], in0=kf[:], scalar=2 * pi, in1=big[:], op0=ALU.mult, op1=ALU.add),
                 reads=["kf", "big"], writes=["big"])
            ts("dve", big[:], big[:], pi, -pi, ALU.min, ALU.max, r=["big"], w=["big"])
            if sgn_col is None:
                S.op("act", lambda e: e.activation(out=dst, in_=big[:], func=AF.Sin), reads=["big"], writes=["ropeC"])
            else:
                S.op("act", lambda e: e.activation(out=big[:], in_=big[:], func=AF.Sin), reads=["big"], writes=["big"])
                ts("dve", dst, big[:], sgn_col, None, ALU.mult, r=["big", "col3"], w=["ropeS"])

        sin_of(C["ropeS"][:], 0.0, col[0:32, 3:4])
        sin_of(C["ropeC"][:], 0.5 * pi, None)

    if which == "attn":
        big = ph.T("big", [128, S_], F32)
        big2 = ph.T("big2", [128, S_], F32)
        w1i = ph.T("w1i", [128, 512], mybir.dt.int32)
        w3i = ph.T("w3i", [128, 512], mybir.dt.int32)
        ts("dve", col[:, 4:5], col[:, 0:1], 64.0, None, ALU.is_ge, r=["col0"], w=["col4"])
        for k, (off, W, dil) in enumerate(keys):
            iota(w1[:], [[1, 512]], off, -1, ["w1"])
            ts("dve", w2[:], w1[:], 0.0, None, ALU.is_ge, r=["w1"], w=["w2"])
            if W is not None:
                ts("dve", w3[:], w1[:], float(W - 1), None, ALU.is_le, r=["w1"], w=["w3"])
                tt("dve", w2[:], w2[:], w3[:], ALU.mult, r=["w2", "w3"], w=["w2"])
            if dil > 1:
                iota(w1i[:], [[1, 512]], off, -1, ["w1i"])
                op_ts(S, "dve", w3i[:], w1i[:], dil - 1, None, ALU.bitwise_and, r=["w1i"], w=["w3i"])
                op_ts(S, "dve", w3[:], w3i[:], 0, None, ALU.is_equal, r=["w3i"], w=["w3"])
                tt("dve", w2[:], w2[:], w3[:], ALU.mult, r=["w2", "w3"], w=["w2"])
            ts("dve", C["masks"][:, k, :], w2[:], -1.0, -NEGB, ALU.add, ALU.mult, r=["w2"], w=["mask%d" % k])
        iota(big[:], [[1, S_]], -31, -16, ["big"])
        ts("dve", big[:], big[:], 0.0, None, ALU.is_ge, r=["big"], w=["big"])
        ts("dve", C["cmask"][:], big[:], -1.0, -NEGB, ALU.add, ALU.mult, r=["big"], w=["cmask"])
        iota(big2[:, :], [[1, S_]], 0, -64, ["big2"])
        ts("dve", big[:, :], big2[:, :], 0.0, None, ALU.is_ge, r=["big2"], w=["big"])
        ts("dve", big2[:, :], big2[:, :], 63.0, None, ALU.is_le, r=["big2"], w=["big2"])
        tt("dve", C["E32"][:], big[:, :], big2[:, :], ALU.mult, r=["big", "big2"], w=["E32"])
        iota(big2[:, :], [[1, S_]], 0, -256, ["big2"])
        ts("dve", big[:, :], big2[:, :], 0.0, None, ALU.is_ge, r=["big2"], w=["big"])
        ts("dve", big2[:, :], big2[:, :], 255.0, None, ALU.is_le, r=["big2"], w=["big2"])
        tt("dve", C["E8"][:], big[:, :], big2[:, :], ALU.mult, r=["big", "big2"], w=["E8"])
        g1 = ph.T("g1", [24, 24, 128], F32)
        iota(g1[:], [[1, 24], [0, 128]], 0, -1, ["g1"])
        ts("dve", C["gsel"][:], g1[:], 0.0, None, ALU.is_equal, r=["g1"], w=["gsel"])
        iota(w1[:, 0:32], [[-64, 32]], 0, 16, ["w1"])
        ts("dve", w2[:, 0:32], w1[:, 0:32], 32.0, 64.0, ALU.add, ALU.min, r=["w1"], w=["w2"])
        ts("dve", w3[:, 0:32], w1[:, 0:32], 0.0, None, ALU.max, r=["w1"], w=["w3"])
        tt("dve", w2[:, 0:32], w2[:, 0:32], w3[:, 0:32], ALU.subtract, r=["w2", "w3"], w=["w2"])
        ts("dve", w2[:, 0:32], w2[:, 0:32], 0.0, 1.0 / 32.0, ALU.max, ALU.mult, r=["w2"], w=["w2"])
        S.op("dve", lambda e: e.memset(w2[:, 32:33], 1.0), reads=["w2"], writes=["w2"])
        S.op("dve", lambda e: e.tensor_copy(out=C["ovx"][:], in_=w2[:, 0:33]), reads=["w2"], writes=["ovx"])
        q1 = ph.T("q1", [128, 8, 32], F32)
        q2 = ph.T("q2", [128, 8, 32], F32)
        q3 = ph.T("q3", [128, 8, 32], F32)
        iota(q1[:], [[2, 8], [-1, 32]], 16, 0, ["q1"])
        ts("dve", q1[:], q1[:], col[:, 4:5], None, ALU.add, r=["q1", "col4"], w=["q1"])
        ts("dve", C["cand"][:], q1[:], 2.0, None, ALU.is_ge, r=["q1"], w=["cand"])
        ts("dve", q2[:], q1[:], 0.0, None, ALU.is_ge, r=["q1"], w=["q2"])
        ts("dve", q3[:], q1[:], 1.0, None, ALU.is_le, r=["q1"], w=["q3"])
        tt("dve", C["forced"][:], q2[:], q3[:], ALU.mult, r=["q2", "q3"], w=["forced"])
        S.op("dve", lambda e: e.memset(C["forced"][:, :, 0:1], 1.0), reads=["forced"], writes=["forced"])
        S.op("dve", lambda e: e.memset(C["cand"][:, :, 0:1], 0.0), reads=["cand"], writes=["cand"])
        ts("dve", C["negbig"][:], C["cand"][:], -1.0, BIG, ALU.add, ALU.mult, r=["cand"], w=["negbig"])
        m1 = ph.T("m1", [128, 16, 8], F32)
        m2 = ph.T("m2", [128, 16, 8], F32)
        iota(m1[:], [[-1, 16], [2, 8]], 0, 0, ["m1"])
        ts("dve", C["pastneg"][:], m1[:], -1.5, -BIG, ALU.is_ge, ALU.mult, r=["m1"], w=["pastneg"])
        ts("dve", m2[:], m1[:], -1.0, None, ALU.is_ge, r=["m1"], w=["m2"])
        ts("dve", C["own"][:], m1[:], 0.0, None, ALU.is_le, r=["m1"], w=["own"])
        tt("dve", C["own"][:], C["own"][:], m2[:], ALU.mult, r=["own", "m2"], w=["own"])
    ph.run()


def mask_ap(C, off, W, dil):
    return C["masks"][:, C["mask_keys"][(off, W, dil)], :]


def op_ts(S, eng, out, in0, s1, s2, op0, op1=None, r=(), w=()):
    if op1 is None:
        S.op(eng, lambda e: e.tensor_scalar(out=out, in0=in0, scalar1=s1, scalar2=None, op0=op0), reads=r, writes=w)
    else:
        S.op(eng, lambda e: e.tensor_scalar(out=out, in0=in0, scalar1=s1, scalar2=s2, op0=op0, op1=op1), reads=r, writes=w)


def op_tt(S, eng, out, in0, in1, op, r=(), w=()):
    S.op(eng, lambda e: e.tensor_tensor(out=out, in0=in0, in1=in1, op=op), reads=r, writes=w)


def op_act(S, out, in_, func, r=(), w=(), **kw):
    S.op("act", lambda e: e.activation(out=out, in_=in_, func=func, **kw), reads=r, writes=w)


def op_mm(S, out, lhsT, rhs, start, stop, r=(), w=()):
    S.op("pe", lambda e: e.matmul(out, lhsT, rhs, start=start, stop=stop), reads=r, writes=w)


def phase_norm(nc, cfg, C, x_src, g_ap, hT, ntt):
    D, KC = cfg.D, cfg.KC
    ph = Phase(nc)
    S = ph.S
    cpb = min(8, KC)
    nbk = KC // cpb
    gT = ph.T("gT", [128, KC], F32)
    S.dma("sp", gT[:], g_ap.rearrange("(k p) -> p k", p=128), writes=["gT"], slow=True)
    xt = [ph.T("xt%d" % i, [128, D], F32) for i in range(2)]
    xn = [ph.T("xn%d" % i, [128, D], BF16) for i in range(2)]
    junk = ph.T("junk", [128, D], BF16)
    st = ph.T("st", [128, ntt, 4], F32)
    ptr = [ph.P("ptr%d" % i, [128, cpb * 128], BF16) for i in range(2 * nbk)]
    S.op("dve", lambda e: e.memset(st[:], 0.0), writes=["st"])
    for tt in range(ntt):
        b = tt % 2
        xb, nb = "xt%d" % b, "xn%d" % b
        S.dma("sp", xt[b][:], x_src[tt * 128:(tt + 1) * 128, :], writes=[xb])
        op_act(S, junk[:], xt[b][:], AF.Square, r=[xb, "st"], w=["junk", "s0_%d" % tt], accum_out=st[:, tt, 0:1])
        op_ts(S, "dve", st[:, tt, 1:2], st[:, tt, 0:1], 1.0 / D, 1e-6, ALU.mult, ALU.add, r=["s0_%d" % tt], w=["s1_%d" % tt])
        op_act(S, st[:, tt, 2:3], st[:, tt, 1:2], AF.Sqrt, r=["s1_%d" % tt], w=["s2_%d" % tt])
        S.op("dve", lambda e, tt=tt: e.reciprocal(out=st[:, tt, 3:4], in_=st[:, tt, 2:3]), reads=["s2_%d" % tt], writes=["s3_%d" % tt])
        op_ts(S, "dve", xn[b][:], xt[b][:], st[:, tt, 3:4], None, ALU.mult, r=[xb, "s3_%d" % tt], w=[nb])
        for k in range(nbk):
            pt = ptr[b * nbk + k]
            pn = "ptr%d" % (b * nbk + k)
            for c in range(cpb):
                kc = k * cpb + c
                S.op("pe", lambda e, pt=pt, c=c, kc=kc, b=b: e.transpose(pt[:, c * 128:(c + 1) * 128], xn[b][:, kc * 128:(kc + 1) * 128], C["ident"][:]),
                     reads=[nb], writes=[pn])
            op_tt(S, "dve", hT[:, k * cpb:(k + 1) * cpb, tt * 128:(tt + 1) * 128],
                  pt[:, :].rearrange("p (k t) -> p k t", t=128),
                  gT[:, k * cpb:(k + 1) * cpb].unsqueeze(2).to_broadcast([128, cpb, 128]), ALU.mult,
                  r=[pn, "gT"], w=["hT"])
    ph.run()


def fm_index(cfg):
    idx = {}
    n = 0
    for name, cnt in (("a_q_raw", 8), ("a_q", 8), ("a_kc", 2), ("a_vc", 2), ("a_ks", 2), ("a_kw", 2),
                      ("b_q", 12), ("b_k", 12), ("c_q", 8), ("c_k", 8),
                      ("m_a", cfg.KC), ("m_b", cfg.KC), ("m_c", cfg.KC)):
        idx[name] = n
        n += cnt
    idx["_n"] = n
    return idx


def p1_spans(cfg):
    spans = []
    fi = fm_index(cfg)

    def fm(name, kind):
        c0, w = cfg.col[name]
        nch = w // 128
        for s in range(0, nch, 4):
            k = min(4, nch - s)
            spans.append(dict(c0=c0 + s * 128, w=k * 128, kind=kind, base=fi[name] + s,
                              rawbase=(fi["a_q_raw"] + s) if kind == "both" else None))

    def tm(name, dst, dcol):
        c0, w = cfg.col[name]
        for s in range(0, w, 512):
            k = min(512, w - s)
            spans.append(dict(c0=c0 + s, w=k, kind="tm", dst=dst, dcol=dcol + s))

    fm("a_q", "both")
    fm("a_kc", "raw")
    fm("a_vc", "raw")
    fm("a_ks", "rot")
    tm("a_vs", "vA", 0)
    fm("a_kw", "rot")
    tm("a_vw", "vA", 256)
    spans.append(dict(c0=cfg.col["a_g"][0], w=24, kind="gate"))
    fm("b_q", "rot")
    fm("b_k", "rot")
    tm("b_v", "vB", 0)
    fm("c_q", "rot")
    fm("c_k", "rot")
    tm("c_v", "vC", 0)
    fm("m_a", "sig")
    fm("m_b", "sig")
    fm("m_c", "sig")
    return spans


def phase_inproj(nc, cfg, C, Dm, l, hT, SL, tok0):
    KC = cfg.KC
    ntb, ntt = SL // 512, SL // 128
    ph = Phase(nc)
    S = ph.S
    w_in = Dm["w_in"]
    NW = 3
    wb = [ph.T("wb%d" % i, [128, KC, 512], BF16) for i in range(NW)]
    stg = [ph.T("stg%d" % i, [128, SL], BF16) for i in range(4)]
    vst = [ph.T("vst%d" % i, [128, 512], BF16) for i in range(3)]
    qtmp = [ph.T("qtmp%d" % i, [32, 512], BF16) for i in range(2)]
    t1 = [ph.T("t1_%d" % i, [32, 512], F32) for i in range(2)]
    t2 = [ph.T("t2_%d" % i, [32, 512], F32) for i in range(2)]
    psm = [ph.P("psm%d" % i, [128, 512], F32) for i in range(4)]
    psw = [ph.P("psw%d" % i, [128, 512], F32) for i in range(2)]
    cnt = dict(ps=0, stg=0, vst=0, rot=0)

    def new_stg():
        i = cnt["stg"] % 4
        cnt["stg"] += 1
        return stg[i], "stg%d" % i

    for si, sp in enumerate(p1_spans(cfg)):
        wi = si % NW
        wt, wn = wb[wi], "wb%d" % wi
        c0, w = sp["c0"], sp["w"]
        S.dma("pool", wt[:, :, 0:w], w_in[l, :, c0:c0 + w].rearrange("(k p) n -> p k n", p=128), writes=[wn])
        kind = sp["kind"]
        if kind == "tm":
            dst = Dm[sp["dst"]]
            for tt in range(ntt):
                pi_ = cnt["ps"] % 4
                cnt["ps"] += 1
                ps, pn = psm[pi_], "psm%d" % pi_
                for kc in range(KC):
                    op_mm(S, ps[:, 0:w], hT[:, kc, tt * 128:(tt + 1) * 128], wt[:, kc, 0:w], kc == 0, kc == KC - 1, r=[wn], w=[pn])
                vi = cnt["vst"] % 3
                cnt["vst"] += 1
                op_act(S, vst[vi][:, 0:w], ps[:, 0:w], AF.Copy, r=[pn], w=["vst%d" % vi])
                S.dma("sp", dst[tok0 + tt * 128:tok0 + (tt + 1) * 128, sp["dcol"]:sp["dcol"] + w], vst[vi][:, 0:w], reads=["vst%d" % vi])
            continue
        if kind == "gate":
            sg, sn = new_stg()
            for tb in range(ntb):
                pi_ = cnt["ps"] % 4
                cnt["ps"] += 1
                ps, pn = psm[pi_], "psm%d" % pi_
                for kc in range(KC):
                    op_mm(S, ps[0:24, :], wt[:, kc, 0:24], hT[:, kc, tb * 512:(tb + 1) * 512], kc == 0, kc == KC - 1, r=[wn], w=[pn])
                op_act(S, sg[0:24, tb * 512:(tb + 1) * 512], ps[0:24, :], AF.Sigmoid, r=[pn], w=[sn + "_%d" % tb])
            S.dma("sp", Dm["agT"][:, tok0:tok0 + SL], sg[0:24, :], reads=[sn + "_%d" % tb for tb in range(ntb)] + [sn], writes=[sn])
            continue
        for ci in range(w // 128):
            sg, sn = new_stg()
            if kind == "both":
                sr, srn = new_stg()
            for tb in range(ntb):
                ts_ = slice(tb * 512, (tb + 1) * 512)
                pi_ = cnt["ps"] % 4
                cnt["ps"] += 1
                ps, pn = psm[pi_], "psm%d" % pi_
                for kc in range(KC):
                    op_mm(S, ps[:, :], wt[:, kc, ci * 128:(ci + 1) * 128], hT[:, kc, ts_], kc == 0, kc == KC - 1, r=[wn], w=[pn])
                if kind == "raw":
                    op_act(S, sg[:, ts_], ps[:, :], AF.Copy, r=[pn], w=[sn + "_%d" % tb])
                elif kind == "sig":
                    op_act(S, sg[:, ts_], ps[:, :], AF.Sigmoid, r=[pn], w=[sn + "_%d" % tb])
                else:
                    ri = cnt["rot"] % 2
                    cnt["rot"] += 1
                    qn, pwn = "qtmp%d" % ri, "psw%d" % ri
                    op_act(S, qtmp[ri][:, :], ps[0:32, :], AF.Copy, r=[pn], w=[qn])
                    if kind == "both":
                        op_act(S, sr[:, ts_], ps[:, :], AF.Copy, r=[pn], w=[srn + "_%d" % tb])
                    op_act(S, sg[:, ts_], ps[:, :], AF.Copy, r=[pn], w=[sn + "_%d" % tb])
                    op_mm(S, psw[ri][0:32, :], C["pm"][:, :], qtmp[ri][:, :], True, True, r=[qn], w=[pwn])
                    gs = slice(tok0 + tb * 512, tok0 + (tb + 1) * 512)
                    op_tt(S, "dve", t1[ri][:, :], psw[ri][0:32, :], C["ropeS"][:, gs], ALU.mult, r=[pwn], w=["t1_%d" % ri])
                    op_tt(S, "dve", t2[ri][:, :], ps[0:32, :], C["ropeC"][:, gs], ALU.mult, r=[pn], w=["t2_%d" % ri])
                    op_tt(S, "dve", sg[0:32, ts_], t1[ri][:, :], t2[ri][:, :], ALU.add, r=["t1_%d" % ri, "t2_%d" % ri], w=[sn + "_%d" % tb])
            allr = [sn] + [sn + "_%d" % tb for tb in range(ntb)] + [sn + "h_%d" % tb for tb in range(ntb)]
            S.dma("sp", Dm["projT"][sp["base"] + ci, :, tok0:tok0 + SL], sg[:, :], reads=allr, writes=allr)
            if kind == "both":
                allr = [srn] + [srn + "_%d" % tb for tb in range(ntb)]
                S.dma("sp", Dm["projT"][sp["rawbase"] + ci, :, tok0:tok0 + SL], sr[:, :], reads=allr, writes=allr)
    ph.run()


class Attn:
    LOOK = 2

    def __init__(self, ph, C):
        self.ph, self.C, self.S = ph, C, ph.S
        self.ps_s = [ph.P("pss%d" % i, [128, 512], F32) for i in range(3)]
        self.ps_o = [ph.P("pso%d" % i, [128, 512], F32) for i in range(2)]
        self.ps_d = [ph.P("psd%d" % i, [128, 512], F32) for i in range(2)]
        self.eb = [ph.T("eb%d" % i, [128, 512], BF16) for i in range(4)]
        self.rec = [ph.T("rec%d" % i, [128, 512], F32) for i in range(2)]
        self.ns = 0
        self.ne = 0
        self.nb = 0
        self.nm = 0
        self.queue = []

    def block(self, tiles, scale, done):
        S = self.S
        ob = self.nb % 2
        self.nb += 1
        n = len(tiles)
        for k, t in enumerate(tiles):
            nk = t["nk"]
            si = self.ns % 3
            self.ns += 1
            pss, pssn = self.ps_s[si], "pss%d" % si
            parts = [(t["k"], t["q"], list(t["reads"]))]
            if t.get("bias") is not None:
                parts.append((t["bias"][0], t["bias"][1], list(t["bias"][2])))
            if t.get("mask") is not None:
                parts.append((self.C["ident"][0:nk, 0:nk], t["mask"], []))
            for pi_, (lh, rh, rd) in enumerate(parts):
                op_mm(S, pss[0:nk, :], lh, rh, pi_ == 0, pi_ == len(parts) - 1, r=rd, w=[pssn])
            ei = self.ne % 4
            self.ne += 1
            eb, ebn = self.eb[ei], "eb%d" % ei
            op_act(S, eb[0:nk, :], pss[0:nk, :], AF.Exp, r=[pssn], w=[ebn], scale=scale)
            if t.get("extra") is not None:
                t["extra"](eb, ebn)
            self.queue.append((t, eb, ebn, k, n, ob, done))
            if len(self.queue) > self.LOOK:
                self._pop()

    def _pop(self):
        S, C = self.S, self.C
        t, eb, ebn, k, n, ob, done = self.queue.pop(0)
        pso, pson = self.ps_o[ob], "pso%d" % ob
        psd, psdn = self.ps_d[ob], "psd%d" % ob
        nk = t["nk"]
        op_mm(S, pso[:, :], t["v"], eb[0:nk, :], k == 0, k == n - 1, r=[ebn] + list(t["reads"]), w=[pson])
        op_mm(S, psd[:, :], C["ones"][0:nk, :], eb[0:nk, :], k == 0, k == n - 1, r=[ebn], w=[psdn])
        if k == n - 1:
            rec, recn = self.rec[ob], "rec%d" % ob
            op_act(S, rec[:, :], psd[:, :], AF.Ln, r=[psdn], w=[recn], bias=C["eps"][:, 0:1])
            op_act(S, rec[:, :], rec[:, :], AF.Exp, r=[recn], w=[recn], scale=-1.0)
            done(pso, pson, rec, recn)

    def flush(self):
        while self.queue:
            self._pop()


def load_fm(S, eng, dst, Dm, idx, w):
    S.dma(eng, dst, Dm["projT"][idx, :, :], writes=w)


def load_v(S, eng, dst, vten, col0, w):
    S.dma(eng, dst, vten[:, col0:col0 + 128].rearrange("(j p) c -> p j c", p=128), writes=w)


def phase_nsa_compress(nc, cfg, C, Dm, l, groups=(0, 1)):
    S_ = cfg.S
    fi = fm_index(cfg)
    ph = Phase(nc)
    S = ph.S
    psg = ph.P("psg", [128, 512], F32)
    psx = ph.P("psx", [128, 512], F32)
    w1 = {}
    w2 = {}
    pe = {}
    for kv in ("k", "v"):
        w1[kv] = ph.T("w1" + kv, [128, 32, 256], BF16)
        w2[kv] = ph.T("w2" + kv, [128, 2, 128], BF16)
        pe[kv] = ph.T("pe" + kv, [128, 32], BF16)
        S.dma("pool", w1[kv][:, :, :], Dm["cmp_w1_" + kv][l].rearrange("l d f -> d l f"), writes=["w1" + kv])
        S.dma("pool", w2[kv][:, :, :], Dm["cmp_w2_" + kv][l].rearrange("(c p) d -> p c d", p=128), writes=["w2" + kv])
        S.dma("pool", pe[kv][:, :], Dm["cmp_pe_" + kv][l].rearrange("l d -> d l"), writes=["pe" + kv], slow=True)
    raw = {kv: ph.T("raw" + kv, [128, S_], BF16) for kv in ("k", "v")}
    hid = {kv: ph.T("hid" + kv, [128, 2, 128], BF16) for kv in ("k", "v")}
    b1 = ph.T("b1", [128, 4], F32)
    zt = ph.T("zt", [128, 128], F32)
    ut = ph.T("ut", [128, 128], F32)
    sgt = ph.T("sgt", [128, 128], F32)
    kcmp = ph.T("kcmp", [128, 128], BF16)
    vcmp = ph.T("vcmp", [128, 128], BF16)
    S.op("dve", lambda e: e.memset(kcmp[:, :], 0.0), writes=["kcmp"])
    S.op("dve", lambda e: e.memset(vcmp[:, :], 0.0), writes=["vcmp"])
    for g in groups:
        load_fm(S, "sp", raw["k"][:, :], Dm, fi["a_kc"] + g, ["rawk"])
        load_fm(S, "sp", raw["v"][:, :], Dm, fi["a_vc"] + g, ["rawv"])
        for kv in ("k", "v"):
            raw3 = raw[kv][:, :].rearrange("p (m s) -> p m s", s=16)
            for fc in range(2):
                for li in range(32):
                    op_mm(S, psx[:, 0:127], w1[kv][:, li, fc * 128:(fc + 1) * 128], raw3[:, li // 16:li // 16 + 127, li % 16],
                          li == 0, li == 31, r=["w1" + kv, "raw" + kv], w=["psx"])
                for li in range(32):
                    op_mm(S, psg[:, 0:1], w1[kv][:, li, fc * 128:(fc + 1) * 128], pe[kv][:, li:li + 1],
                          li == 0, li == 31, r=["w1" + kv, "pe" + kv], w=["psg"])
                op_act(S, b1[:, 0:1], psg[:, 0:1], AF.Copy, r=["psg"], w=["b1"])
                op_act(S, zt[:, 0:127], psx[:, 0:127], AF.Identity, r=["psx", "b1"], w=["zt"], bias=b1[:, 0:1])
                op_tt(S, "dve", ut[:, 0:127], zt[:, 0:127], zt[:, 0:127], ALU.mult, r=["zt"], w=["ut"])
                op_ts(S, "dve", ut[:, 0:127], ut[:, 0:127], 0.044715, 1.0, ALU.mult, ALU.add, r=["ut"], w=["ut"])
                op_tt(S, "dve", ut[:, 0:127], ut[:, 0:127], zt[:, 0:127], ALU.mult, r=["ut", "zt"], w=["ut"])
                op_act(S, sgt[:, 0:127], ut[:, 0:127], AF.Sigmoid, r=["ut"], w=["sgt"], scale=1.5957691216057308)
                op_tt(S, "dve", hid[kv][:, fc, 0:127], zt[:, 0:127], sgt[:, 0:127], ALU.mult, r=["zt", "sgt"], w=["hid" + kv])
        for fc in range(2):
            op_mm(S, psx[:, 0:127], w2["k"][:, fc, :], hid["k"][:, fc, 0:127], fc == 0, fc == 1, r=["w2k", "hidk"], w=["psx"])
        op_act(S, kcmp[:, 0:127], psx[:, 0:127], AF.Copy, r=["psx"], w=["kcmp"])
        for fc in range(2):
            op_mm(S, psx[0:127, 0:128], hid["v"][:, fc, 0:127], w2["v"][:, fc, :], fc == 0, fc == 1, r=["w2v", "hidv"], w=["psx"])
        op_act(S, vcmp[0:127, :], psx[0:127, 0:128], AF.Copy, r=["psx"], w=["vcmp"])
        S.dma("sp", Dm["cmpK"][g, :, :], kcmp[:, :], reads=["kcmp"])
        S.dma("sp", Dm["cmpV"][g, :, :], vcmp[:, :], reads=["vcmp"])
    ph.run()


def phase_nsa(nc, cfg, C, Dm, l, groups=(0, 1)):
    S_ = cfg.S
    NTB, NTT = S_ // 512, S_ // 128
    fi = fm_index(cfg)
    scale = HD ** -0.5
    ph = Phase(nc)
    S = ph.S
    A = Attn(ph, C)
    psg = ph.P("psg", [128, 512], F32)
    psx = psg
    ag = ph.T("ag", [24, S_], BF16)
    S.dma("sp", ag[:, :], Dm["agT"][:, :], writes=["ag"])
    kcmp = ph.T("kcmp", [128, 128], BF16)
    vcmp = ph.T("vcmp", [128, 128], BF16)
    ks = ph.T("ks", [128, S_], BF16)
    kw = ph.T("kw", [128, S_], BF16)
    vs = ph.T("vs", [128, NTT, 128], BF16)
    vw = ph.T("vw", [128, NTT, 128], BF16)
    qraw = [ph.T("qraw%d" % i, [128, S_], BF16) for i in range(2)]
    qrot = [ph.T("qrot%d" % i, [128, S_], BF16) for i in range(2)]
    acc = ph.T("acc", [128, 4, S_], F32)
    impsum = ph.T("impsum", [128, 8, 32], F32)
    rcol = ph.T("rcol", [128, 4], F32)
    BT = ph.T("BT", [128, S_ - 1024], BF16)
    S.op("dve", lambda e: e.memset(BT[:, :], 0.0), writes=["BT"])
    coef = [ph.T("coef%d" % i, [128, 512], F32) for i in range(2)]
    tmp = [ph.T("tmp%d" % i, [128, 512], F32) for i in range(2)]
    scw = ph.T("scw", [128, 32], F32)
    wk = ph.T("wk", [128, 32], F32)
    m8 = ph.T("m8", [128, 16], F32)
    selt = ph.T("selt", [128, 32], F32)
    biasb = ph.T("biasb", [128, 32], BF16)
    ostg = [ph.T("ostg%d" % i, [128, S_], BF16) for i in range(2)]
    cnt = dict(c=0)

    gb = [ph.T("gb%d" % i, [128, 3, S_], BF16) for i in range(2)]

    def gate_bcast(h, br):
        for I in range(NTB):
            qs = slice(I * 512, (I + 1) * 512)
            op_mm(S, psg[:, :], C["gsel"][0:24, 3 * h + br, :], ag[0:24, qs], True, True, r=["ag"], w=["psg"])
            op_act(S, gb[h % 2][:, br, qs], psg[:, :], AF.Copy, r=["psg"], w=["gb%d_%d" % (h % 2, br)])

    def combine(pso, pson, rec, recn, h4, I, row, first):
        qs = slice(I * 512, (I + 1) * 512)
        h, br = divmod(row, 3)
        ci = cnt["c"] % 2
        cnt["c"] += 1
        op_tt(S, "dve", coef[ci][:, :], gb[h % 2][:, br, qs], rec[:, :], ALU.mult, r=["gb%d_%d" % (h % 2, br), recn], w=["coef%d" % ci])
        an = "acc%d_%d" % (h4, I)
        if first:
            op_tt(S, "dve", acc[:, h4, qs], pso[:, :], coef[ci][:, :], ALU.mult, r=[pson, "coef%d" % ci], w=[an])
        else:
            op_tt(S, "dve", tmp[ci][:, :], pso[:, :], coef[ci][:, :], ALU.mult, r=[pson, "coef%d" % ci], w=["tmp%d" % ci])
            op_tt(S, "pool", acc[:, h4, qs], acc[:, h4, qs], tmp[ci][:, :], ALU.add, r=[an, "tmp%d" % ci], w=[an])

    for g in groups:
        S.dma("sp", kcmp[:, :], Dm["cmpK"][g, :, :], writes=["kcmp"])
        S.dma("sp", vcmp[:, :], Dm["cmpV"][g, :, :], writes=["vcmp"])
        load_fm(S, "sp", ks[:, :], Dm, fi["a_ks"] + g, ["ks"])
        load_fm(S, "sp", kw[:, :], Dm, fi["a_kw"] + g, ["kw"])
        load_v(S, "sp", vs[:, :, :], Dm["vA"], g * 128, ["vs"])
        load_v(S, "sp", vw[:, :, :], Dm["vA"], 256 + g * 128, ["vw"])
        heads = [4 * g + r for r in range(4)]
        for r4, h in enumerate(heads):
            qb = h % 2
            load_fm(S, "sp", qraw[qb][:, :], Dm, fi["a_q_raw"] + h, ["qraw%d" % qb])
            gate_bcast(h, 0)
            for I in range(NTB):
                qs = slice(I * 512, (I + 1) * 512)

                def extra(eb, ebn, I=I, r4=r4):
                    if I < 2:
                        return
                    for t4 in range(4):
                        tl = 4 * I + t4 - 8
                        op_mm(S, psx[:, t4 * 64:t4 * 64 + 33], eb[0:127, t4 * 128:(t4 + 1) * 128], C["ovx"][0:127, 0:33], True, True,
                              r=[ebn], w=["psg"])
                    for t4 in range(4):
                        tl = 4 * I + t4 - 8
                        op_ts(S, "dve", rcol[:, t4:t4 + 1], psx[:, t4 * 64 + 32:t4 * 64 + 33], 1e-30, None, ALU.max, r=["psg"], w=["rcol"])
                        S.op("dve", lambda e, t4=t4: e.reciprocal(out=rcol[:, t4:t4 + 1], in_=rcol[:, t4:t4 + 1]), reads=["rcol"], writes=["rcol"])
                        if r4 == 0:
                            op_ts(S, "dve", impsum[:, tl, :], psx[:, t4 * 64:t4 * 64 + 32], rcol[:, t4:t4 + 1], None, ALU.mult,
                                  r=["psg", "rcol"], w=["imp%d" % tl])
                        else:
                            S.op("dve", lambda e, t4=t4, tl=tl: e.scalar_tensor_tensor(
                                out=impsum[:, tl, :], in0=psx[:, t4 * 64:t4 * 64 + 32], scalar=rcol[:, t4:t4 + 1],
                                in1=impsum[:, tl, :], op0=ALU.mult, op1=ALU.add), reads=["psg", "rcol", "imp%d" % tl], writes=["imp%d" % tl])

                tiles = [dict(q=qraw[qb][:, qs], k=kcmp[:, 0:127], v=vcmp[0:127, :], nk=127,
                              reads=["qraw%d" % qb, "kcmp", "vcmp"], mask=C["cmask"][0:127, qs], extra=extra)]
                A.block(tiles, scale, lambda pso, pson, rec, recn, r4=r4, I=I, h=h: combine(pso, pson, rec, recn, r4, I, 3 * h + 0, True))

        for tl in range(NTT - 8):
            op_tt(S, "dve", scw[:, :], impsum[:, tl, :], C["cand"][:, tl, :], ALU.mult, r=["imp%d" % tl], w=["scw"])
            op_tt(S, "dve", scw[:, :], scw[:, :], C["negbig"][:, tl, :], ALU.add, r=["scw"], w=["scw"])
            S.op("dve", lambda e: e.max(out=m8[:, 0:8], in_=scw[:, :]), reads=["scw"], writes=["m8a"])
            S.op("dve", lambda e: e.match_replace(out=wk[:, :], in_to_replace=m8[:, 0:8], in_values=scw[:, :], imm_value=-BIG),
                 reads=["scw", "m8a"], writes=["wk"])
            S.op("dve", lambda e: e.max(out=m8[:, 8:16], in_=wk[:, :]), reads=["wk"], writes=["m8b"])
            op_ts(S, "dve", selt[:, :], scw[:, :], m8[:, 12:13], None, ALU.is_ge, r=["scw", "m8b"], w=["selt"])
            op_tt(S, "dve", selt[:, :], selt[:, :], C["forced"][:, tl, :], ALU.max, r=["selt"], w=["selt"])
            op_ts(S, "dve", biasb[:, :], selt[:, :], -1.0, -NEGB, ALU.add, ALU.mult, r=["selt"], w=["biasb"])
            op_mm(S, psx[0:32, 0:128], biasb[:, 0:32], C["ident"][:, :], True, True, r=["biasb"], w=["psg"])
            op_act(S, BT[0:32, tl * 128:(tl + 1) * 128], psx[0:32, 0:128], AF.Copy, r=["psg"], w=["BT"])

        for r4, h in enumerate(heads):
            qb = h % 2
            load_fm(S, "sp", qrot[qb][:, :], Dm, fi["a_q"] + h, ["qrot%d" % qb])
            gate_bcast(h, 1)
            gate_bcast(h, 2)
            for I in range(NTB):
                qs = slice(I * 512, (I + 1) * 512)
                tiles = []
                for j in range(4 * I + 4):
                    t = dict(q=qrot[qb][:, qs], k=ks[:, j * 128:(j + 1) * 128], v=vs[:, j, :], nk=128,
                             reads=["qrot%d" % qb, "ks", "vs"])
                    if j >= 4 * I:
                        t["mask"] = mask_ap(C, 512 * I - 128 * j, None, 1)
                    if I >= 2:
                        t["bias"] = (C["E32"][:, j * 128:(j + 1) * 128], BT[:, (I - 2) * 512:(I - 1) * 512], ["BT"])
                    tiles.append(t)
                A.block(tiles, scale, lambda pso, pson, rec, recn, r4=r4, I=I, h=h: combine(pso, pson, rec, recn, r4, I, 3 * h + 1, False))
                tiles = []
                for j in range(max(0, 4 * I - 4), 4 * I + 4):
                    off = 512 * I - 128 * j
                    tiles.append(dict(q=qrot[qb][:, qs], k=kw[:, j * 128:(j + 1) * 128], v=vw[:, j, :], nk=128,
                                      reads=["qrot%d" % qb, "kw", "vw"],
                                      mask=mask_ap(C, off, None, 1) if off <= 0 else mask_ap(C, off, 512, 1)))

                def fin(pso, pson, rec, recn, r4=r4, I=I, h=h):
                    combine(pso, pson, rec, recn, r4, I, 3 * h + 2, False)
                    if I == NTB - 1:
                        oi = h % 2
                        accn = ["acc%d_%d" % (r4, I2) for I2 in range(NTB)]
                        S.op("act", lambda e: e.activation(out=ostg[oi][:, :], in_=acc[:, r4, :], func=AF.Copy),
                             reads=accn, writes=["ostg%d" % oi])
                        S.dma("sp", Dm["oT"][h, :, :], ostg[oi][:, :], reads=["ostg%d" % oi], writes=["ostg%d" % oi])

                A.block(tiles, scale, fin)
        A.flush()
    ph.run()


def phase_dil(nc, cfg, C, Dm, l, slots=(0, 1, 2, 3)):
    S_ = cfg.S
    NTB, NTT = S_ // 512, S_ // 128
    fi = fm_index(cfg)
    scale = HD ** -0.5
    ph = Phase(nc)
    S = ph.S
    A = Attn(ph, C)
    q = [[ph.T("q%d_%d" % (p, g), [128, S_], BF16) for g in range(3)] for p in range(2)]
    k = [[ph.T("k%d_%d" % (p, g), [128, S_], BF16) for g in range(3)] for p in range(2)]
    v = [[ph.T("v%d_%d" % (p, g), [128, NTT, 128], BF16) for g in range(3)] for p in range(2)]
    ostg = [ph.T("ostg%d" % i, [128, S_], BF16) for i in range(2)]
    for si, j in enumerate(slots):
        p = si % 2
        for g in range(3):
            hh = 4 * g + j
            load_fm(S, "sp", q[p][g][:, :], Dm, fi["b_q"] + hh, ["q%d_%d" % (p, g)])
            load_fm(S, "sp", k[p][g][:, :], Dm, fi["b_k"] + hh, ["k%d_%d" % (p, g)])
            load_v(S, "sp", v[p][g][:, :, :], Dm["vB"], hh * 128, ["v%d_%d" % (p, g)])
        for I in range(NTB):
            qs = slice(I * 512, (I + 1) * 512)
            tiles = []
            for g, (W, dil, lo) in enumerate(((129, 1, 4 * I - 1), (513, 4, 4 * I - 4), (None, 16, 0))):
                for jj in range(max(0, lo), 4 * I + 4):
                    off = 512 * I - 128 * jj
                    if g == 2 and off > 0:
                        off = 4096
                    tiles.append(dict(q=q[p][g][:, qs], k=k[p][g][:, jj * 128:(jj + 1) * 128], v=v[p][g][:, jj, :], nk=128,
                                      reads=["q%d_%d" % (p, g), "k%d_%d" % (p, g), "v%d_%d" % (p, g)],
                                      mask=mask_ap(C, off, W, dil)))

            def fin(pso, pson, rec, recn, p=p, I=I, j=j, qs=qs):
                op_tt(S, "dve", ostg[p][:, qs], pso[:, :], rec[:, :], ALU.mult, r=[pson, recn], w=["ostg%d_%d" % (p, I)])
                if I == NTB - 1:
                    names = ["ostg%d_%d" % (p, I2) for I2 in range(NTB)]
                    S.dma("sp", Dm["oT"][8 + j, :, :], ostg[p][:, :], reads=names, writes=names)

            A.block(tiles, scale, fin)
    A.flush()
    ph.run()


def phase_moba(nc, cfg, C, Dm, l, heads=tuple(range(8))):
    S_ = cfg.S
    NTB, NTT = S_ // 512, S_ // 128
    NBK = S_ // 256
    fi = fm_index(cfg)
    scale = HD ** -0.5
    ph = Phase(nc)
    S = ph.S
    A = Attn(ph, C)
    psx = ph.P("psx", [128, 512], F32)
    q = [ph.T("q%d" % p, [128, S_], BF16) for p in range(2)]
    k = [ph.T("k%d" % p, [128, S_], BF16) for p in range(2)]
    v = [ph.T("v%d" % p, [128, NTT, 128], BF16) for p in range(2)]
    ostg = [ph.T("ostg%d" % i, [128, S_], BF16) for i in range(2)]
    km = ph.T("km", [128, 8], F32)
    kmb = ph.T("kmb", [128, 8], BF16)
    gm = ph.T("gm", [128, NTT, 8], F32)
    m8 = ph.T("m8", [128, NTT, 8], F32)
    sel = ph.T("sel", [128, NTT, 8], F32)
    okt = ph.T("okt", [128, NTT, 8], F32)
    biasb = ph.T("biasb", [128, NTT, 8], BF16)
    BT = ph.T("BT", [128, S_], BF16)
    S.op("dve", lambda e: e.memset(BT[:, :], 0.0), writes=["BT%d" % I for I in range(NTB)])
    for hi, h in enumerate(heads):
        p = hi % 2
        qn, kn, vn = "q%d" % p, "k%d" % p, "v%d" % p
        load_fm(S, "sp", q[p][:, :], Dm, fi["c_q"] + h, [qn])
        load_fm(S, "sp", k[p][:, :], Dm, fi["c_k"] + h, [kn])
        load_v(S, "sp", v[p][:, :, :], Dm["vC"], h * 128, [vn])
        S.op("dve", lambda e, p=p: e.reduce_sum(out=km[:, 0:NBK], in_=k[p][:, :].rearrange("p (b s) -> p b s", s=256), axis=AX.X),
             reads=[kn], writes=["km"])
        op_ts(S, "dve", kmb[:, 0:NBK], km[:, 0:NBK], 1.0 / 256.0, None, ALU.mult, r=["km"], w=["kmb"])
        for tt in range(NTT):
            op_mm(S, psx[:, tt * 8:tt * 8 + NBK], q[p][:, tt * 128:(tt + 1) * 128], kmb[:, 0:NBK], True, True, r=[qn, "kmb"], w=["psx"])
        op_tt(S, "dve", gm[:, :, 0:NBK], psx[:, 0:NTT * 8].rearrange("p (t j) -> p t j", j=8)[:, :, 0:NBK], C["pastneg"][:, 0:NTT, 0:NBK], ALU.add,
              r=["psx"], w=["gm"])
        for tt in range(NTT):
            S.op("dve", lambda e, tt=tt: e.max(out=m8[:, tt, :], in_=gm[:, tt, :]), reads=["gm"], writes=["m8"])
        op_tt(S, "dve", sel[:, :, :], gm[:, :, :], m8[:, :, 2:3].to_broadcast([128, NTT, 8]), ALU.is_ge, r=["gm", "m8"], w=["sel"])
        op_ts(S, "dve", okt[:, :, :], gm[:, :, :], -0.5 * BIG, None, ALU.is_gt, r=["gm"], w=["okt"])
        op_tt(S, "dve", sel[:, :, :], sel[:, :, :], okt[:, :, :], ALU.mult, r=["sel", "okt"], w=["sel"])
        op_tt(S, "dve", sel[:, :, :], sel[:, :, :], C["own"][:, 0:NTT, :], ALU.max, r=["sel"], w=["sel"])
        op_ts(S, "dve", biasb[:, :, :], sel[:, :, :], -1.0, -NEGB, ALU.add, ALU.mult, r=["sel"], w=["biasb"])
        for I in range(NTB):
            for t4 in range(4):
                tt = 4 * I + t4
                op_mm(S, psx[0:8, t4 * 128:(t4 + 1) * 128], biasb[:, tt, :], C["ident"][:, :], True, True, r=["biasb"], w=["psx"])
            op_act(S, BT[0:8, I * 512:(I + 1) * 512], psx[0:8, :], AF.Copy, r=["psx"], w=["BT%d" % I])
        for I in range(NTB):
            qs = slice(I * 512, (I + 1) * 512)
            tiles = []
            for j in range(4 * I + 4):
                t = dict(q=q[p][:, qs], k=k[p][:, j * 128:(j + 1) * 128], v=v[p][:, j, :], nk=128, reads=[qn, kn, vn],
                         bias=(C["E8"][:, j * 128:(j + 1) * 128], BT[:, qs], ["BT%d" % I]))
                if j >= 4 * I:
                    t["mask"] = mask_ap(C, 512 * I - 128 * j, None, 1)
                tiles.append(t)

            def fin(pso, pson, rec, recn, p=p, I=I, h=h, qs=qs):
                op_tt(S, "dve", ostg[p][:, qs], pso[:, :], rec[:, :], ALU.mult, r=[pson, recn], w=["ostg%d_%d" % (p, I)])
                if I == NTB - 1:
                    names = ["ostg%d_%d" % (p, I2) for I2 in range(NTB)]
                    S.dma("sp", Dm["oT"][12 + h, :, :], ostg[p][:, :], reads=names, writes=names)

            A.block(tiles, scale, fin)
    A.flush()
    ph.run()


def phase_merge(nc, cfg, C, Dm, l, SL):
    D, KC = cfg.D, cfg.KC
    ntb = SL // 512
    fi = fm_index(cfg)
    ph = Phase(nc)
    S = ph.S
    wbr = ph.T("wbr", [128, 20, D], BF16)
    S.dma("pool", wbr[:, 0:8, :], Dm["w_br_a"][l].rearrange("(h p) d -> p h d", p=128), writes=["wbr"])
    S.dma("pool", wbr[:, 8:12, :], Dm["w_br_b"][l].rearrange("(h p) d -> p h d", p=128), writes=["wbr"])
    S.dma("pool", wbr[:, 12:20, :], Dm["w_br_c"][l].rearrange("(h p) d -> p h d", p=128), writes=["wbr"])
    oblk = [ph.T("oblk%d" % i, [128, 20, 512], BF16) for i in range(2)]
    mg = [ph.T("mg%d" % i, [128, 3, 512], BF16) for i in range(2)]
    mstg = [ph.T("mstg%d" % i, [128, KC, 512], BF16) for i in range(2)]
    ta = [ph.T("ta%d" % i, [128, 512], F32) for i in range(2)]
    tb_ = [ph.T("tb%d" % i, [128, 512], F32) for i in range(2)]
    ps = [[ph.P("ps%d_%d" % (i, j), [128, 512], F32) for j in range(3)] for i in range(2)]
    n = 0
    for tb in range(ntb):
        ob = tb % 2
        ts_ = slice(tb * 512, (tb + 1) * 512)
        S.dma("sp", oblk[ob][:, :, :], Dm["oT"][:, :, ts_].rearrange("h p t -> p h t"), writes=["oblk%d" % ob])
        msn = "mstg%d" % ob
        for dc in range(KC):
            b = n % 2
            n += 1
            for j, nm in enumerate(("m_a", "m_b", "m_c")):
                S.dma("sp", mg[b][:, j, :], Dm["projT"][fi[nm] + dc, :, ts_], writes=["mg%d_%d" % (b, j)])
            for j, (h0, h1) in enumerate(((0, 8), (8, 12), (12, 20))):
                for h in range(h0, h1):
                    op_mm(S, ps[b][j][:, :], wbr[:, h, dc * 128:(dc + 1) * 128], oblk[ob][:, h, :], h == h0, h == h1 - 1,
                          r=["wbr", "oblk%d" % ob], w=["ps%d_%d" % (b, j)])
            op_tt(S, "dve", ta[b][:, :], ps[b][0][:, :], mg[b][:, 0, :], ALU.mult, r=["ps%d_0" % b, "mg%d_0" % b], w=["ta%d" % b])
            op_tt(S, "dve", tb_[b][:, :], ps[b][1][:, :], mg[b][:, 1, :], ALU.mult, r=["ps%d_1" % b, "mg%d_1" % b], w=["tb%d" % b])
            op_tt(S, "pool", ta[b][:, :], ta[b][:, :], tb_[b][:, :], ALU.add, r=["ta%d" % b, "tb%d" % b], w=["ta%d" % b])
            op_tt(S, "dve", tb_[b][:, :], ps[b][2][:, :], mg[b][:, 2, :], ALU.mult, r=["ps%d_2" % b, "mg%d_2" % b], w=["tb%d" % b])
            op_tt(S, "pool", mstg[ob][:, dc, :], ta[b][:, :], tb_[b][:, :], ALU.add, r=["ta%d" % b, "tb%d" % b], w=[msn])
        S.dma("sp", Dm["mergedT"][:, :, ts_].rearrange("k p t -> p k t"), mstg[ob][:, :, :], reads=[msn], writes=[msn])
    ph.run()


def phase_wo(nc, cfg, C, Dm, l, x_src, x_dst, SL):
    D, KC = cfg.D, cfg.KC
    ntt = SL // 128
    CW = min(512, D)
    ph = Phase(nc)
    S = ph.S
    wo = ph.T("wo", [128, KC, D], BF16)
    S.dma("pool", wo[:, :, :], Dm["w_o"][l].rearrange("(k p) n -> p k n", p=128), writes=["wo"])
    mblk = [ph.T("mblk%d" % i, [128, KC, 512], BF16) for i in range(2)]
    xt = [ph.T("xt%d" % i, [128, D], F32) for i in range(3)]
    ps = [ph.P("ps%d" % i, [128, 512], F32) for i in range(4)]
    n = 0

    def load(tt):
        tb, t4 = divmod(tt, 4)
        if t4 == 0:
            S.dma("sp", mblk[tb % 2][:, :, :], Dm["mergedT"][:, :, tb * 512:(tb + 1) * 512].rearrange("k p t -> p k t"),
                  writes=["mblk%d" % (tb % 2)])
        S.dma("sp", xt[tt % 3][:, :], x_src[tt * 128:(tt + 1) * 128, :], writes=["xt%d" % (tt % 3)])

    load(0)
    for tt in range(ntt):
        if tt + 1 < ntt:
            load(tt + 1)
        tb, t4 = divmod(tt, 4)
        mb, xb = tb % 2, tt % 3
        for cb in range(D // CW):
            pi_ = n % 4
            n += 1
            for kc in range(KC):
                op_mm(S, ps[pi_][:, 0:CW], mblk[mb][:, kc, t4 * 128:(t4 + 1) * 128], wo[:, kc, cb * CW:(cb + 1) * CW],
                      kc == 0, kc == KC - 1, r=["wo", "mblk%d" % mb], w=["ps%d" % pi_])
            op_tt(S, "dve", xt[xb][:, cb * CW:(cb + 1) * CW], xt[xb][:, cb * CW:(cb + 1) * CW], ps[pi_][:, 0:CW], ALU.add,
                  r=["xt%d" % xb, "ps%d" % pi_], w=["xt%d" % xb])
        S.dma("sp", x_dst[tt * 128:(tt + 1) * 128, :], xt[xb][:, :], reads=["xt%d" % xb], writes=["xt%d" % xb])
    ph.run()


def phase_mlp_in(nc, cfg, C, Dm, l, hT, SL):
    KC, FC = cfg.KC, cfg.FC
    ntb, ntt = SL // 512, SL // 128
    ph = Phase(nc)
    S = ph.S
    NW = 3
    w1b = [ph.T("w1b%d" % i, [128, KC, 512], BF16) for i in range(NW)]
    ustg = [ph.T("ustg%d" % i, [128, SL], BF16) for i in range(3)]
    rt = [ph.T("rt%d" % i, [128, 512], F32) for i in range(3)]
    psi = [ph.P("psi%d" % i, [128, 512], F32) for i in range(4)]
    n2 = n3 = 0
    for fs in range(FC // 4):
        wi = fs % NW
        S.dma("pool", w1b[wi][:, :, :], Dm["w_mlp_in"][l, :, fs * 512:(fs + 1) * 512].rearrange("(k p) n -> p k n", p=128),
              writes=["w1b%d" % wi])
        for ci in range(4):
            fc = 4 * fs + ci
            ui = fc % 3
            un = "ustg%d" % ui
            for tb in range(ntb):
                ts_ = slice(tb * 512, (tb + 1) * 512)
                pi_ = n2 % 4
                n2 += 1
                for kc in range(KC):
                    op_mm(S, psi[pi_][:, :], w1b[wi][:, kc, ci * 128:(ci + 1) * 128], hT[:, kc, ts_], kc == 0, kc == KC - 1,
                          r=["w1b%d" % wi], w=["psi%d" % pi_])
                ri = n3 % 3
                n3 += 1
                op_act(S, rt[ri][:, :], psi[pi_][:, :], AF.Relu, r=["psi%d" % pi_], w=["rt%d" % ri])
                op_tt(S, "dve", ustg[ui][:, ts_], rt[ri][:, :], rt[ri][:, :], ALU.mult, r=["rt%d" % ri], w=[un + "_%d" % tb])
            names = [un + "_%d" % tb for tb in range(ntb)]
            S.dma("sp", Dm["uS"][:, :, fc, :].rearrange("t p c -> p t c"), ustg[ui][:, :].rearrange("p (t c) -> p t c", c=128),
                  reads=names, writes=names)
    ph.run()


def phase_mlp_out(nc, cfg, C, Dm, l, xres, SL):
    D, FC = cfg.D, cfg.FC
    ntt = SL // 128
    CW = min(512, D)
    FQ = max(1, FC // 4)
    ph = Phase(nc)
    S = ph.S
    w2c = [ph.T("w2c%d" % i, [128, FC, CW], BF16) for i in range(2)]
    ub = [ph.T("ub%d" % i, [128, FC, 128], BF16) for i in range(3)]
    xs = [ph.T("xs%d" % i, [128, CW], F32) for i in range(3)]
    pso = [ph.P("pso%d" % i, [128, 512], F32) for i in range(4)]
    items = [(cb, tt) for cb in range(D // CW) for tt in range(ntt)]

    def load(i):
        cb, tt = items[i]
        wi = cb % 2
        if tt == 0:
            for q4 in range(FC // FQ):
                S.dma("pool", w2c[wi][:, q4 * FQ:(q4 + 1) * FQ, :],
                      Dm["w_mlp_out"][l, q4 * FQ * 128:(q4 + 1) * FQ * 128, cb * CW:(cb + 1) * CW].rearrange("(f p) n -> p f n", p=128),
                      writes=["w2c%d_%d" % (wi, q4)])
        S.dma("sp", ub[i % 3][:, :, :], Dm["uS"][tt, :, :, :], writes=["ub%d" % (i % 3)])
        S.dma("sp", xs[i % 3][:, :], xres[tt * 128:(tt + 1) * 128, cb * CW:(cb + 1) * CW], writes=["xs%d" % (i % 3)])

    load(0)
    for i, (cb, tt) in enumerate(items):
        if i + 1 < len(items):
            load(i + 1)
        wi, ui, pi_ = cb % 2, i % 3, i % 4
        for fc in range(FC):
            op_mm(S, pso[pi_][:, 0:CW], ub[ui][:, fc, :], w2c[wi][:, fc, :], fc == 0, fc == FC - 1,
                  r=["ub%d" % ui, "w2c%d_%d" % (wi, fc // FQ)], w=["pso%d" % pi_])
        op_tt(S, "dve", xs[ui][:, :], xs[ui][:, :], pso[pi_][:, 0:CW], ALU.add, r=["xs%d" % ui, "pso%d" % pi_], w=["xs%d" % ui])
        S.dma("sp", xres[tt * 128:(tt + 1) * 128, cb * CW:(cb + 1) * CW], xs[ui][:, :], reads=["xs%d" % ui], writes=["xs%d" % ui])
    ph.run()


def phase_final(nc, cfg, C, x_src, g_ap, out_ap, SL):
    D = cfg.D
    ntt = SL // 128
    ph = Phase(nc)
    S = ph.S
    gB = ph.T("gB", [128, D], F32)
    S.dma("sp", gB[:, :], g_ap.partition_broadcast(128), writes=["gB"])
    xt = [ph.T("xt%d" % i, [128, D], F32) for i in range(3)]
    junk = ph.T("junk", [128, D], BF16)
    st = ph.T("st", [128, ntt, 4], F32)
    S.op("dve", lambda e: e.memset(st[:], 0.0), writes=["st"])
    S.dma("sp", xt[0][:, :], x_src[0:128, :], writes=["xt0"])
    for tt in range(ntt):
        b = tt % 3
        xb = "xt%d" % b
        if tt + 1 < ntt:
            S.dma("sp", xt[(tt + 1) % 3][:, :], x_src[(tt + 1) * 128:(tt + 2) * 128, :], writes=["xt%d" % ((tt + 1) % 3)])
        op_act(S, junk[:, :], xt[b][:, :], AF.Square, r=[xb, "st"], w=["junk", "s0_%d" % tt], accum_out=st[:, tt, 0:1])
        op_ts(S, "dve", st[:, tt, 1:2], st[:, tt, 0:1], 1.0 / D, 1e-6, ALU.mult, ALU.add, r=["s0_%d" % tt], w=["s1_%d" % tt])
        op_act(S, st[:, tt, 2:3], st[:, tt, 1:2], AF.Sqrt, r=["s1_%d" % tt], w=["s2_%d" % tt])
        S.op("dve", lambda e, tt=tt: e.reciprocal(out=st[:, tt, 3:4], in_=st[:, tt, 2:3]), reads=["s2_%d" % tt], writes=["s3_%d" % tt])
        op_ts(S, "dve", xt[b][:, :], xt[b][:, :], st[:, tt, 3:4], None, ALU.mult, r=[xb, "s3_%d" % tt], w=[xb])
        op_tt(S, "pool", xt[b][:, :], xt[b][:, :], gB[:, :], ALU.mult, r=[xb, "gB"], w=[xb])
        S.dma("sp", out_ap[tt * 128:(tt + 1) * 128, :], xt[b][:, :], reads=[xb], writes=[xb])
    ph.run()


INPUT_SHAPES = lambda c: {
    "x": [c.S, c.D],
    "attn_norm_g": [c.DEPTH, c.D],
    "w_in": [c.DEPTH, c.D, c.INW],
    "cmp_pe_k": [c.DEPTH, 32, 128], "cmp_w1_k": [c.DEPTH, 32, 128, 256], "cmp_w2_k": [c.DEPTH, 256, 128],
    "cmp_pe_v": [c.DEPTH, 32, 128], "cmp_w1_v": [c.DEPTH, 32, 128, 256], "cmp_w2_v": [c.DEPTH, 256, 128],
    "w_br_a": [c.DEPTH, 1024, c.D], "w_br_b": [c.DEPTH, 512, c.D], "w_br_c": [c.DEPTH, 1024, c.D],
    "w_o": [c.DEPTH, c.D, c.D],
    "mlp_norm_g": [c.DEPTH, c.D],
    "w_mlp_in": [c.DEPTH, c.D, c.DFF], "w_mlp_out": [c.DEPTH, c.DFF, c.D],
    "final_norm_g": [c.D],
}


def build(cfg, stop_after=None):
    nc = bass.Bass("TRN2", target_bir_lowering=False)
    Dm = {}
    for name, shp in INPUT_SHAPES(cfg).items():
        Dm[name] = nc.dram_tensor(name, shp, F32, kind="ExternalInput").ap()
    out = nc.dram_tensor("out", [cfg.S, cfg.D], F32, kind="ExternalOutput").ap()
    fi = fm_index(cfg)
    S_, D = cfg.S, cfg.D
    Dm["xres"] = nc.dram_tensor("xres", [S_, D], F32).ap()
    Dm["projT"] = nc.dram_tensor("projT", [fi["_n"], 128, S_], BF16).ap()
    Dm["agT"] = nc.dram_tensor("agT", [24, S_], BF16).ap()
    Dm["vA"] = nc.dram_tensor("vA", [S_, 512], BF16).ap()
    Dm["vB"] = nc.dram_tensor("vB", [S_, 1536], BF16).ap()
    Dm["vC"] = nc.dram_tensor("vC", [S_, 1024], BF16).ap()
    Dm["cmpK"] = nc.dram_tensor("cmpK", [2, 128, 128], BF16).ap()
    Dm["cmpV"] = nc.dram_tensor("cmpV", [2, 128, 128], BF16).ap()
    Dm["oT"] = nc.dram_tensor("oT", [20, 128, S_], BF16).ap()
    Dm["mergedT"] = nc.dram_tensor("mergedT", [cfg.KC, 128, S_], BF16).ap()
    Dm["uS"] = nc.dram_tensor("uS", [S_ // 128, 128, cfg.FC, 128], BF16).ap()
    C = {}
    with contextlib.ExitStack() as gst:
        build_consts(nc, cfg, C, gst, "global")
        for l in range(cfg.DEPTH):
            x_src = Dm["x"] if l == 0 else Dm["xres"]
            with contextlib.ExitStack() as st:
                hT = st.enter_context(nc.sbuf_tensor("hT_a%d" % l, [128, cfg.KC, S_], BF16))
                build_consts(nc, cfg, C, st, "rope")
                phase_norm(nc, cfg, C, x_src, Dm["attn_norm_g"][l], hT, S_ // 128)
                phase_inproj(nc, cfg, C, Dm, l, hT, S_, 0)
            if stop_after == "inproj":
                break
            with contextlib.ExitStack() as st:
                build_consts(nc, cfg, C, st, "attn")
                phase_nsa_compress(nc, cfg, C, Dm, l)
                phase_nsa(nc, cfg, C, Dm, l)
                phase_dil(nc, cfg, C, Dm, l)
                phase_moba(nc, cfg, C, Dm, l)
            if stop_after == "attn":
                break
            phase_merge(nc, cfg, C, Dm, l, S_)
            phase_wo(nc, cfg, C, Dm, l, x_src, Dm["xres"], S_)
            with contextlib.ExitStack() as st:
                hT = st.enter_context(nc.sbuf_tensor("hT_m%d" % l, [128, cfg.KC, S_], BF16))
                phase_norm(nc, cfg, C, Dm["xres"], Dm["mlp_norm_g"][l], hT, S_ // 128)
                phase_mlp_in(nc, cfg, C, Dm, l, hT, S_)
            phase_mlp_out(nc, cfg, C, Dm, l, Dm["xres"], S_)
        if stop_after is None:
            phase_final(nc, cfg, C, Dm["xres"], Dm["final_norm_g"], out, S_)
    return nc


_NC_CACHE = {}


def kernel(**inputs):
    cfg = Cfg()
    x = np.asarray(inputs["x"], dtype=np.float32)
    B = x.shape[0]
    if "nc" not in _NC_CACHE:
        _NC_CACHE["nc"] = build(cfg)
    nc = _NC_CACHE["nc"]
    shared = {k: np.ascontiguousarray(np.asarray(inputs[k], dtype=np.float32)) for k in INPUT_SHAPES(cfg) if k != "x"}
    n_cores = 8
    in_maps = []
    zeros = np.zeros_like(x[0])
    for c in range(n_cores):
        m = dict(shared)
        m["x"] = np.ascontiguousarray(x[c // 2]) if (c % 2 == 0 and c // 2 < B) else zeros
        in_maps.append(m)
    res = run_bass_kernel_spmd(nc, in_maps, core_ids=list(range(n_cores)))
    return np.stack([np.asarray(res.results[2 * b]["out"], dtype=np.float32) for b in range(B)], axis=0)
```
